# Optimizing a Trainium2 kernel written in Bass

```python
import math
import jax, jax.numpy as jnp
from jax import lax
import numpy as np

D_MODEL = 1024
BATCH = 8
SEQ = 8192
DEPTH = 2
DEC_BATCH = 32
DEC_SEQ = 64
PAST_LEN = 2048

CHUNK = 64
N_MIXERS = 2
N_CONV_LAYERS = (DEPTH + 1) // 2
N_GDN_LAYERS = DEPTH // 2
CONV_WIDTH = 31
GDN_QK_HEADS = 8
GDN_V_HEADS = 16
GDN_HEAD_DIM = 128
GDN_QK_WIDTH = GDN_QK_HEADS * GDN_HEAD_DIM
GDN_V_WIDTH = GDN_V_HEADS * GDN_HEAD_DIM
GDN_CONV_CH = 2 * GDN_QK_WIDTH + GDN_V_WIDTH
GDN_IN_WIDTH = GDN_CONV_CH + GDN_V_WIDTH + 2 * GDN_V_HEADS
SHORT_CONV_WIDTH = 4
FFN_HIDDEN = ((8 * D_MODEL + 3 * 256 - 1) // (3 * 256)) * 256
EPS = 1e-6

kernel_name = "hybrid_conformer_gdn_stream_step"


def rms_norm(x, g):
    xf = x.astype(jnp.float32)
    y = xf * lax.rsqrt(jnp.mean(xf * xf, axis=-1, keepdims=True) + EPS)
    return (y * g.astype(jnp.float32)).astype(x.dtype)


def layer_norm(x, g, b):
    xf = x.astype(jnp.float32)
    mu = jnp.mean(xf, axis=-1, keepdims=True)
    var = jnp.mean(jnp.square(xf - mu), axis=-1, keepdims=True)
    y = (xf - mu) * lax.rsqrt(var + EPS)
    return (y * g.astype(jnp.float32) + b.astype(jnp.float32)).astype(x.dtype)


def l2_normalize(x):
    xf = x.astype(jnp.float32)
    return xf * lax.rsqrt(jnp.sum(xf * xf, axis=-1, keepdims=True) + EPS)


def causal_depthwise_conv(ext, w):
    return lax.conv_general_dilated(
        ext, w[:, None, :].astype(ext.dtype), window_strides=(1,), padding='VALID',
        dimension_numbers=('NWC', 'WIO', 'NWC'), feature_group_count=ext.shape[-1])


def modulation(c, w_ada, b_ada):
    mod = jax.nn.silu(c) @ w_ada + b_ada
    return jnp.split(mod[:, None, :], 6, axis=-1)


def conformer_conv(h, buf, w_pw1, b_pw1, w_dw, b_dw, ln_g, ln_b, w_pw2, b_pw2):
    a = h @ w_pw1 + b_pw1
    u = a[..., :D_MODEL] * jax.nn.sigmoid(a[..., D_MODEL:])
    ext = jnp.concatenate([buf.astype(u.dtype), u], axis=1)
    y = causal_depthwise_conv(ext, w_dw) + b_dw
    y = jax.nn.silu(layer_norm(y, ln_g, ln_b))
    return y @ w_pw2 + b_pw2, ext[:, -(CONV_WIDTH - 1):]


def gated_delta_rule(q, k, v, g, beta, S0, chunk):
    B, L, H, DK = q.shape
    DV = v.shape[-1]
    n = L // chunk

    def blk(t):
        t = t.astype(jnp.float32).reshape((B, n, chunk) + t.shape[2:])
        return jnp.moveaxis(t, 2, 3)

    q, k, v, g, beta = blk(q), blk(k), blk(v), blk(g), blk(beta)
    gc = jnp.cumsum(g, axis=-1)
    pos = jnp.arange(chunk)
    incl = pos[:, None] >= pos[None, :]
    strict = pos[:, None] > pos[None, :]
    decay = jnp.exp(jnp.where(incl, gc[..., :, None] - gc[..., None, :], -jnp.inf))
    kb = k * beta[..., None]
    m = jnp.where(strict, jnp.einsum('bnhcd,bnhsd->bnhcs', kb, k) * decay, 0.0)
    t_mat = m + jnp.eye(chunk, dtype=jnp.float32)
    rhs = jnp.concatenate([v * beta[..., None], kb * jnp.exp(gc)[..., None]], axis=-1)
    sol = lax.linalg.triangular_solve(t_mat, rhs, left_side=True, lower=True, unit_diagonal=True)
    u, w = sol[..., :DV], sol[..., DV:]
    qk = jnp.einsum('bnhcd,bnhsd->bnhcs', q, k) * decay
    q_dec = q * jnp.exp(gc)[..., None]
    k_dec = k * jnp.exp(gc[..., -1:] - gc)[..., None]
    g_tot = jnp.exp(gc[..., -1])

    def step(S, xs):
        u_c, w_c, qk_c, qd_c, kd_c, gt_c = xs
        v_new = u_c - jnp.einsum('bhcd,bhdv->bhcv', w_c, S)
        o = jnp.einsum('bhcd,bhdv->bhcv', qd_c, S) + jnp.einsum('bhcs,bhsv->bhcv', qk_c, v_new)
        S = S * gt_c[..., None, None] + jnp.einsum('bhcd,bhcv->bhdv', kd_c, v_new)
        return S, o

    xs = tuple(jnp.moveaxis(t, 1, 0) for t in (u, w, qk, q_dec, k_dec, g_tot))
    S, o = lax.scan(step, S0.astype(jnp.float32), xs)
    o = jnp.moveaxis(jnp.moveaxis(o, 0, 1), 3, 2).reshape(B, L, H, DV)
    return o, S


def gdn_mixer(h, conv_buf, S0, w_in, w_conv, A_log, dt_bias, g_norm, w_out, chunk):
    B, L, _ = h.shape
    p = h @ w_in
    qkv, z, a, b = jnp.split(p, [GDN_CONV_CH, GDN_CONV_CH + GDN_V_WIDTH,
                                 GDN_CONV_CH + GDN_V_WIDTH + GDN_V_HEADS], axis=-1)
    ext = jnp.concatenate([conv_buf.astype(qkv.dtype), qkv], axis=1)
    new_buf = ext[:, -(SHORT_CONV_WIDTH - 1):]
    qkv = jax.nn.silu(causal_depthwise_conv(ext, w_conv))
    q, k, v = jnp.split(qkv, [GDN_QK_WIDTH, 2 * GDN_QK_WIDTH], axis=-1)
    rep = GDN_V_HEADS // GDN_QK_HEADS
    q = jnp.repeat(l2_normalize(q.reshape(B, L, GDN_QK_HEADS, GDN_HEAD_DIM)), rep, axis=2) * (GDN_HEAD_DIM ** -0.5)
    k = jnp.repeat(l2_normalize(k.reshape(B, L, GDN_QK_HEADS, GDN_HEAD_DIM)), rep, axis=2)
    v = v.reshape(B, L, GDN_V_HEADS, GDN_HEAD_DIM)
    beta = jax.nn.sigmoid(b.astype(jnp.float32))
    g = -jnp.exp(A_log.astype(jnp.float32)) * jax.nn.softplus(a.astype(jnp.float32) + dt_bias.astype(jnp.float32))
    o, S = gated_delta_rule(q, k, v, g, beta, S0, chunk)
    o = rms_norm(o, g_norm) * jax.nn.silu(z.astype(jnp.float32).reshape(B, L, GDN_V_HEADS, GDN_HEAD_DIM))
    out = o.reshape(B, L, GDN_V_WIDTH).astype(h.dtype) @ w_out
    return out, new_buf, S.astype(S0.dtype)


def trunk(x, c, conv_bufs, gdn_states, gdn_bufs, p, chunk):
    new_conv, new_S, new_gbuf = [], [], []
    for i in range(DEPTH):
        sh1, sc1, gt1, sh2, sc2, gt2 = modulation(c, p['w_ada'][i], p['b_ada'][i])
        h = rms_norm(x, p['g_pre_mix'][i]) * (1 + sc1) + sh1
        j = i // N_MIXERS
        if i % N_MIXERS == 0:
            m, buf = conformer_conv(h, conv_bufs[j], p['w_pw1'][j], p['b_pw1'][j], p['w_dw'][j], p['b_dw'][j],
                                    p['ln_conv_g'][j], p['ln_conv_b'][j], p['w_pw2'][j], p['b_pw2'][j])
            new_conv.append(buf)
        else:
            m, buf, S = gdn_mixer(h, gdn_bufs[j], gdn_states[j], p['w_gdn_in'][j], p['w_gdn_conv'][j],
                                  p['gdn_A_log'][j], p['gdn_dt_bias'][j], p['g_gdn_out_norm'][j],
                                  p['w_gdn_out'][j], chunk)
            new_gbuf.append(buf)
            new_S.append(S)
        x = x + gt1 * rms_norm(m, p['g_post_mix'][i])
        h = rms_norm(x, p['g_pre_ffn'][i]) * (1 + sc2) + sh2
        f = (jax.nn.silu(h @ p['w_ffn_gate'][i]) * (h @ p['w_ffn_up'][i])) @ p['w_ffn_down'][i]
        x = x + gt2 * rms_norm(f, p['g_post_ffn'][i])
    return x, jnp.stack(new_conv), jnp.stack(new_S), jnp.stack(new_gbuf)


def setup_inputs(seed: int = 0) -> dict:
    key = jax.random.key(seed)
    ks = iter(jax.random.split(key, 48))
    D, NA, NB, F = D_MODEL, N_CONV_LAYERS, N_GDN_LAYERS, FFN_HIDDEN

    def nrm(shape, scale):
        return jax.random.normal(next(ks), shape, jnp.float32) * scale

    def gain(shape):
        return 1.0 + nrm(shape, 0.1)

    dt = jnp.exp(jax.random.uniform(next(ks), (NB, GDN_V_HEADS), jnp.float32,
                                    minval=math.log(1e-3), maxval=math.log(1e-1)))
    return {
        'x_prompt': nrm((BATCH, SEQ, D), 1.0),
        'x_sample': nrm((DEC_BATCH, DEC_SEQ, D), 1.0),
        'c_prompt': nrm((BATCH, D), 1.0),
        'c_sample': nrm((DEC_BATCH, D), 1.0),
        'cache_conv': nrm((NA, DEC_BATCH, CONV_WIDTH - 1, D), 0.5),
        'state_gdn': nrm((NB, DEC_BATCH, GDN_V_HEADS, GDN_HEAD_DIM, GDN_HEAD_DIM), 0.05),
        'cache_gdn_conv': nrm((NB, DEC_BATCH, SHORT_CONV_WIDTH - 1, GDN_CONV_CH), 1.0),
        'w_ada': nrm((DEPTH, D, 6 * D), 0.5 * D ** -0.5),
        'b_ada': nrm((DEPTH, 6 * D), 0.02),
        'g_pre_mix': gain((DEPTH, D)),
        'g_post_mix': gain((DEPTH, D)),
        'g_pre_ffn': gain((DEPTH, D)),
        'g_post_ffn': gain((DEPTH, D)),
        'w_ffn_gate': nrm((DEPTH, D, F), D ** -0.5),
        'w_ffn_up': nrm((DEPTH, D, F), D ** -0.5),
        'w_ffn_down': nrm((DEPTH, F, D), F ** -0.5),
        'w_pw1': nrm((NA, D, 2 * D), D ** -0.5),
        'b_pw1': nrm((NA, 2 * D), 0.02),
        'w_dw': nrm((NA, CONV_WIDTH, D), CONV_WIDTH ** -0.5),
        'b_dw': nrm((NA, D), 0.02),
        'ln_conv_g': gain((NA, D)),
        'ln_conv_b': nrm((NA, D), 0.02),
        'w_pw2': nrm((NA, D, D), D ** -0.5),
        'b_pw2': nrm((NA, D), 0.02),
        'w_gdn_in': nrm((NB, D, GDN_IN_WIDTH), D ** -0.5),
        'w_gdn_conv': nrm((NB, SHORT_CONV_WIDTH, GDN_CONV_CH), SHORT_CONV_WIDTH ** -0.5),
        'gdn_A_log': jnp.log(jax.random.uniform(next(ks), (NB, GDN_V_HEADS), jnp.float32, minval=1.0, maxval=16.0)),
        'gdn_dt_bias': dt + jnp.log(-jnp.expm1(-dt)),
        'g_gdn_out_norm': gain((NB, GDN_HEAD_DIM)),
        'w_gdn_out': nrm((NB, GDN_V_WIDTH, D), GDN_V_WIDTH ** -0.5),
    }


def reference(x_prompt, x_sample, c_prompt, c_sample, cache_conv, state_gdn, cache_gdn_conv,
              w_ada, b_ada, g_pre_mix, g_post_mix, g_pre_ffn, g_post_ffn,
              w_ffn_gate, w_ffn_up, w_ffn_down,
              w_pw1, b_pw1, w_dw, b_dw, ln_conv_g, ln_conv_b, w_pw2, b_pw2,
              w_gdn_in, w_gdn_conv, gdn_A_log, gdn_dt_bias, g_gdn_out_norm, w_gdn_out):
    p = dict(w_ada=w_ada, b_ada=b_ada, g_pre_mix=g_pre_mix, g_post_mix=g_post_mix,
             g_pre_ffn=g_pre_ffn, g_post_ffn=g_post_ffn,
             w_ffn_gate=w_ffn_gate, w_ffn_up=w_ffn_up, w_ffn_down=w_ffn_down,
             w_pw1=w_pw1, b_pw1=b_pw1, w_dw=w_dw, b_dw=b_dw, ln_conv_g=ln_conv_g, ln_conv_b=ln_conv_b,
             w_pw2=w_pw2, b_pw2=b_pw2, w_gdn_in=w_gdn_in, w_gdn_conv=w_gdn_conv, gdn_A_log=gdn_A_log,
             gdn_dt_bias=gdn_dt_bias, g_gdn_out_norm=g_gdn_out_norm, w_gdn_out=w_gdn_out)
    dt = x_prompt.dtype
    conv0 = jnp.zeros((N_CONV_LAYERS, BATCH, CONV_WIDTH - 1, D_MODEL), dt)
    S0 = jnp.zeros((N_GDN_LAYERS, BATCH, GDN_V_HEADS, GDN_HEAD_DIM, GDN_HEAD_DIM), state_gdn.dtype)
    gconv0 = jnp.zeros((N_GDN_LAYERS, BATCH, SHORT_CONV_WIDTH - 1, GDN_CONV_CH), dt)
    y_prompt, conv_p, S_p, gconv_p = trunk(x_prompt, c_prompt, conv0, S0, gconv0, p, CHUNK)
    y_sample, conv_s, S_s, gconv_s = trunk(x_sample, c_sample, cache_conv, state_gdn, cache_gdn_conv,
                                           p, x_sample.shape[1])
    return (y_prompt, y_sample, conv_p, conv_s, S_p, S_s, gconv_p, gconv_s)
```

```python
from contextlib import ExitStack
import numpy as np
import concourse.bass as bass
import concourse.mybir as mybir
from concourse.bass_utils import run_bass_kernel_spmd

F32 = mybir.dt.float32
BF16 = mybir.dt.bfloat16
AF = mybir.ActivationFunctionType
ALU = mybir.AluOpType
AX = mybir.AxisListType
EPOCH = 24000

D = 1024
FF = 2816
NH = 16
HD = 128
CW = 31
SCW = 4
EPS = 1e-6
SEQ = 8192
DSEQ = 64
NSMP = 4
TP = 512
NSLOT = 3
NA_INFLIGHT = 2
NDG = 8
SEQ_EMIT = False
SLOT_E = 4096
CONV_CH = 128 * 8192


class Buf:
    __slots__ = ("name", "w", "r", "excl")

    def __init__(self, name="", excl=False):
        self.name = name
        self.w = None
        self.r = {}
        self.excl = excl


def bufs(n, name=""):
    return [Buf(f"{name}{i}") for i in range(n)]


class DSem:
    def __init__(self, h):
        self.h = h
        self.count = 0


class Eng:
    def __init__(self, fw, name, e):
        self.fw = fw
        self.name = name
        self.e = e
        self.count = 0
        self.sems = []
        self.known = {}
        self.known_dma = {}

    def sem_for(self, n):
        idx = (n - 1) // EPOCH
        while len(self.sems) <= idx:
            self.sems.append(self.fw.new_sem(f"{self.name}{len(self.sems)}"))
        return self.sems[idx], n - idx * EPOCH


class FW:
    def __init__(self):
        self.nc = bass.Bass("TRN2", target_bir_lowering=False)
        self.es = ExitStack()
        nc = self.nc
        self.pe = Eng(self, "pe", nc.tensor)
        self.act = Eng(self, "act", nc.scalar)
        self.dve = Eng(self, "dve", nc.vector)
        self.pool = Eng(self, "pool", nc.gpsimd)
        self.sp = Eng(self, "sp", nc.sync)
        self.nsem = 0
        self.ntile = 0

    def new_sem(self, name):
        self.nsem += 1
        return self.es.enter_context(self.nc.semaphore(f"s{self.nsem}_{name}"))

    def dsem(self, name="d"):
        return DSem(self.new_sem(name))

    def sb(self, name, shape, dt):
        self.ntile += 1
        return self.es.enter_context(self.nc.sbuf_tensor(f"{name}_{self.ntile}", list(shape), dt))

    def ps(self, name, shape, dt=F32):
        self.ntile += 1
        return self.es.enter_context(self.nc.psum_tensor(f"{name}_{self.ntile}", list(shape), dt))

    def _wait(self, E, evs):
        for ev in evs:
            if ev is None:
                continue
            if ev[0] == "eng":
                _, oname, seq = ev
                if oname == E.name and E.name in ("pe", "sp"):
                    continue
                if E.known.get(oname, 0) >= seq:
                    continue
                O = getattr(self, oname)
                sem, val = O.sem_for(seq)
                E.e.wait_ge(sem, val)
                E.known[oname] = seq
            else:
                _, ds, val = ev
                if E.known_dma.get(id(ds), 0) >= val:
                    continue
                E.e.wait_ge(ds.h, val)
                E.known_dma[id(ds)] = val

    @staticmethod
    def _deps(reads, writes):
        evs = []
        for b in reads:
            if b.w is not None:
                evs.append(b.w)
            if b.excl:
                evs.extend(b.r.values())
        for b in writes:
            if b.w is not None:
                evs.append(b.w)
            evs.extend(b.r.values())
        return evs

    def op(self, E, fn, reads=(), writes=(), inc=True):
        self._wait(E, self._deps(reads, writes))
        ins = fn()
        if inc:
            E.count += 1
            sem, _ = E.sem_for(E.count)
            ins.then_inc(sem, 1)
            ev = ("eng", E.name, E.count)
        else:
            ev = ("eng", E.name, E.count + 1)
        for b in reads:
            b.r[E.name] = ev
        for b in writes:
            b.w = ev
            b.r = {}
        return ins

    def dma(self, Q, out_ap, in_ap, ds, reads=(), writes=(), **kw):
        self._wait(Q, self._deps(reads, writes))
        ins = Q.e.dma_start(out=out_ap, in_=in_ap, **kw)
        ds.count += 16
        ins.then_inc(ds.h, 16)
        ev = ("dma", ds, ds.count)
        for b in reads:
            b.r[("dma", id(ds))] = ev
        for b in writes:
            b.w = ev
            b.r = {}
        return ins

    def finish(self, blist):
        evs = []
        for b in blist:
            if b.w is not None:
                evs.append(b.w)
            evs.extend(b.r.values())
        for E in (self.pe, self.act, self.dve, self.pool):
            if E.count:
                evs.append(("eng", E.name, E.count))
        self._wait(self.sp, evs)


def tile_block_list():
    bl = []
    for b in range(4):
        bl.append(("pw1", b, 8, 512))
    for b in range(2):
        bl.append(("pw2", b, 8, 512))
    for b in range(11):
        bl.append(("gu0", b, 8, 512))
    for b in range(8):
        bl.append(("dn0", b, 22, 128))
    bl.append(("gab", 0, 8, 32))
    for g in range(4):
        for i in range(3):
            bl.append(("gin", g * 3 + i, 8, 512))
    for b in range(4):
        bl.append(("gout", b, 16, 256))
    for b in range(11):
        bl.append(("gu1", b, 8, 512))
    for b in range(8):
        bl.append(("dn1", b, 22, 128))
    return bl


def ada_block_list():
    return [("ada", l * 12 + b, 8, 512) for l in range(2) for b in range(12)]


def block_offsets():
    offs = {}
    o = 0
    for blk in ada_block_list() + tile_block_list():
        offs[(blk[0], blk[1])] = o
        o += 128 * blk[2] * blk[3]
    total = ((o + CONV_CH - 1) // CONV_CH) * CONV_CH
    return offs, total


def _arr(Wc):
    K, n = Wc.shape
    KC = K // 128
    return np.ascontiguousarray(Wc.reshape(KC, 128, n).transpose(1, 0, 2)).reshape(-1)


def build_wall(inp):
    offs, total = block_offsets()
    wall = np.zeros(total, np.float32)

    def put(key, Wc):
        a = _arr(np.asarray(Wc, np.float32))
        wall[offs[key]:offs[key] + a.size] = a

    for l in range(2):
        for b in range(12):
            put(("ada", l * 12 + b), inp["w_ada"][l][:, 512 * b:512 * b + 512])
    w1 = inp["w_pw1"][0]
    for b in range(4):
        put(("pw1", b), np.concatenate([w1[:, 256 * b:256 * b + 256], w1[:, 1024 + 256 * b:1024 + 256 * b + 256]], 1))
    for b in range(2):
        put(("pw2", b), inp["w_pw2"][0][:, 512 * b:512 * b + 512])
    for l in range(2):
        g, u, d = inp["w_ffn_gate"][l], inp["w_ffn_up"][l], inp["w_ffn_down"][l]
        for b in range(11):
            put((f"gu{l}", b), np.concatenate([g[:, 256 * b:256 * b + 256], u[:, 256 * b:256 * b + 256]], 1))
        for b in range(8):
            put((f"dn{l}", b), d[:, 128 * b:128 * b + 128])
    wi = inp["w_gdn_in"][0]
    put(("gab", 0), wi[:, 6144:6176])
    for g in range(4):
        put(("gin", g * 3 + 0), np.concatenate([wi[:, 256 * g:256 * g + 256], wi[:, 1024 + 256 * g:1024 + 256 * g + 256]], 1))
        put(("gin", g * 3 + 1), wi[:, 2048 + 512 * g:2048 + 512 * g + 512])
        put(("gin", g * 3 + 2), wi[:, 4096 + 512 * g:4096 + 512 * g + 512])
    wo = inp["w_gdn_out"][0]
    for b in range(4):
        put(("gout", b), wo[:, 256 * b:256 * b + 256])
    return wall


VC = {}
_o = 0
for _n, _w in [("g_pre_mix", 16), ("g_post_mix", 16), ("g_pre_ffn", 16), ("g_post_ffn", 16), ("b_ada", 96),
               ("b_pw1", 16), ("b_dw", 8), ("ln_g", 8), ("ln_b", 8), ("b_pw2", 8), ("w_dw", 248), ("w_gc", 128),
               ("g_norm", 1)]:
    VC[_n] = _o
    _o += _w
NVEC = _o


def fm(v):
    v = np.asarray(v, np.float32)
    return np.ascontiguousarray(v.reshape(-1, 128).T)


def build_vecs(inp):
    V = np.zeros((128, NVEC), np.float32)
    for n in ["g_pre_mix", "g_post_mix", "g_pre_ffn", "g_post_ffn"]:
        for l in range(2):
            V[:, VC[n] + 8 * l:VC[n] + 8 * l + 8] = fm(inp[n][l])
    for l in range(2):
        V[:, VC["b_ada"] + 48 * l:VC["b_ada"] + 48 * l + 48] = fm(inp["b_ada"][l])
    V[:, VC["b_pw1"]:VC["b_pw1"] + 16] = fm(inp["b_pw1"][0])
    V[:, VC["b_dw"]:VC["b_dw"] + 8] = fm(inp["b_dw"][0])
    V[:, VC["ln_g"]:VC["ln_g"] + 8] = fm(inp["ln_conv_g"][0])
    V[:, VC["ln_b"]:VC["ln_b"] + 8] = fm(inp["ln_conv_b"][0])
    V[:, VC["b_pw2"]:VC["b_pw2"] + 8] = fm(inp["b_pw2"][0])
    wd = np.asarray(inp["w_dw"][0], np.float32)
    V[:, VC["w_dw"]:VC["w_dw"] + 248] = wd.reshape(CW, 8, 128).transpose(2, 1, 0).reshape(128, 248)
    wg = np.asarray(inp["w_gdn_conv"][0], np.float32)
    V[:, VC["w_gc"]:VC["w_gc"] + 128] = wg.reshape(SCW, 32, 128).transpose(2, 1, 0).reshape(128, 128)
    V[:, VC["g_norm"]] = np.asarray(inp["g_gdn_out_norm"][0], np.float32)
    return V


def build_masks(C):
    L = int(np.log2(C))
    M = np.zeros((128, 2, L, C), np.float32)
    i = np.arange(C)[:, None]
    j = np.arange(C)[None, :]
    for l in range(L):
        b = 1 << l
        same = (i // (2 * b)) == (j // (2 * b))
        M[:C, 0, l, :] = same & ((i % (2 * b)) < b) & ((j % (2 * b)) >= b)
        M[:C, 1, l, :] = same & ((i % (2 * b)) >= b) & ((j % (2 * b)) < b)
    return M.reshape(128, 2 * L * C)


def build_consts():
    C = np.zeros((128, 5, 128), np.float32)
    s = np.arange(128)[:, None]
    c = np.arange(128)[None, :]
    C[:, 0, :] = (s == c)
    C[:, 1, :] = 1.0
    C[:, 2, :] = (s <= c)
    C[:, 3, :] = np.where(c >= s, 0.0, -1e30)
    C[:, 4, :] = (c > s)
    return C.reshape(128, 640)


class StopBuild(Exception):
    pass


class Prog:
    def __init__(self, n_ptiles=16, do_sample=True, C=128, stop=None):
        self.stop = stop
        self.n_ptiles = n_ptiles
        self.do_sample = do_sample
        self.C = C
        self.fw = FW()
        self.nc = self.fw.nc
        self.build()

    def V(self, name, col, n=1):
        c0 = VC[name] + col
        return self.vecs[:, c0:c0 + n]

    def build(self):
        fw, nc = self.fw, self.nc
        offs, total = block_offsets()
        self.offs = offs
        di = lambda n, s: nc.dram_tensor(n, list(s), F32, kind="ExternalInput")
        do = lambda n, s: nc.dram_tensor(n, list(s), F32, kind="ExternalOutput")
        self.xp = di("xp", [SEQ, D])
        self.xs = di("xs", [NSMP * DSEQ, D])
        self.c5 = di("c5", [1 + NSMP, D])
        self.cconv = di("cconv", [NSMP, CW - 1, D])
        self.sgdn = di("sgdn", [NSMP, NH, HD, HD])
        self.cgc = di("cgc", [NSMP, SCW - 1, 4096])
        self.wall = di("wall", [total])
        self.vecs_d = di("vecs", [128, NVEC])
        self.hrow_d = di("hrow", [1, 32])
        self.consts_d = di("consts", [128, 640])
        self.masks_d = di("masks", [128, 2 * 7 * 128])
        self.yp = do("yp", [SEQ, D])
        self.ys = do("ys", [NSMP * DSEQ, D])
        self.o_ccp = do("ccp", [CW - 1, D])
        self.o_ccs = do("ccs", [NSMP, CW - 1, D])
        self.o_sp = do("sp", [NH, HD, HD])
        self.o_ss = do("ss", [NSMP, NH, HD, HD])
        self.o_gcp = do("gcp", [SCW - 1, 4096])
        self.o_gcs = do("gcs", [NSMP, SCW - 1, 4096])
        self.wsc = nc.dram_tensor("wsc", [total], BF16, kind="Internal")
        self.outbufs = []

        sb = fw.sb
        self.cst = sb("cst", [128, 640], F32)
        self.cbf = sb("cbf", [128, 384], BF16)
        self.vecs = sb("vecs", [128, NVEC], F32)
        self.hrow = sb("hrow", [128, 32], F32)
        self.sc = sb("sc", [128, 8], F32)
        self.coef = sb("coef", [128, 2 * 6 * 8 * 5], F32)
        self.xT = sb("xT", [128, 8, TP], F32)
        self.hT = sb("hT", [128, 8, TP], BF16)
        self.tmp = sb("tmp", [128, 8, TP], F32)
        self.CH = sb("CH", [128, 28, TP], BF16)
        self.sq = self.CH[:, 20:28, :]
        self.u = sb("u", [128, 8, CW - 1 + TP], BF16)
        self.S = sb("S", [128, NH, HD], F32)
        self.wslot = [sb(f"ws{i}", [128, SLOT_E], BF16) for i in range(NSLOT)]
        self.xst = [sb(f"xst{i}", [128, D], F32) for i in range(2)]
        self.ucf = self.xst[1][:, 0:NSMP * 8 * (CW - 1)].rearrange("p (s c h) -> p s c h", s=NSMP, c=8)
        self.ost = [sb(f"ost{i}", [128, D], F32) for i in range(1)]
        self.rs = sb("rs", [128, 2, TP], F32)
        self.dg = self.rs[:].bitcast(BF16).rearrange("p a b -> p (a b)")[:, 0:NDG * 128].rearrange("p (s c) -> p s c", s=NDG)
        self.modt = self.rs[:].rearrange("p a b -> p (a b)")[:, 0:480].rearrange("p (a s) -> p a s", s=5)
        self.sg = [sb(f"sg{i}", [128, TP], F32) for i in range(2)]
        self.ghalo = sb("ghalo", [128, 1 + NSMP, 32, SCW - 1], F32)
        self.b_dg = bufs(NDG)
        self.cstg = [sb(f"cstg{i}", [128, 544], F32) for i in range(2)]
        self.cvo = self.sg
        self.qkf = [sb(f"qkf{i}", [128, TP], F32) for i in range(2)]
        self.sq1 = [sb(f"sq1{i}", [128, TP], BF16) for i in range(2)]
        self.scT = sb("scT", [128, 8, 8], BF16)
        self.cin = self.ost[0][0:32, :]
        self.c5s = self.cin

        B = Buf
        self.b_cst, self.b_cbf, self.b_vecs, self.b_hrow, self.b_sc, self.b_coef = B(), B(), B(), B(), B(), B()
        self.b_xT = bufs(8)
        self.b_hT = bufs(8)
        self.b_tmp = bufs(8)
        self.b_CH = bufs(28)
        self.b_sq = self.b_CH[20:28]
        self.b_u = bufs(8)
        self.b_S = bufs(4)
        self.b_ws = bufs(NSLOT)
        self.d_ws = [fw.dsem("ws") for _ in range(NSLOT)]
        self.b_xst = bufs(2)
        self.b_ucf = [self.b_xst[1]] * NSMP
        self.d_xst = [fw.dsem("xst") for _ in range(2)]
        self.b_ost = bufs(1)
        self.b_cin = self.b_ost[0]
        self.d_ost = [fw.dsem("ost") for _ in range(1)]
        self.b_rs = bufs(2)
        self.b_sg = bufs(2)
        self.b_ghalo = [bufs(32) for _ in range(1 + NSMP)]
        self.b_cstg = bufs(2)
        self.b_cvo = self.b_sg
        self.b_qkf = bufs(2)
        self.b_sq1 = bufs(2)
        self.d_cin = fw.dsem("cin")
        self.d_misc = [fw.dsem("misc") for _ in range(6)]
        self.d_yst = [fw.dsem("yst") for _ in range(4)]
        self.d_S = [fw.dsem("S") for _ in range(4)]

        self.P = [fw.ps(f"P{i}", [128, 512], F32) for i in range(7)]
        self.PB = fw.ps("PB", [128, 1024], BF16)
        self.b_P = [Buf(f"P{i}", excl=True) for i in range(7)]
        self.b_PB = Buf("PB", excl=True)
        self.b_PB2 = self.b_PB

        self.gdn_alloc()
        self.dg2i = 0
        self.b_dg2 = bufs(8)

        self.load_consts()
        self.convert_weights(total)
        self.wq = ada_block_list()
        ntiles = self.n_ptiles + (1 if self.do_sample else 0)
        tb = tile_block_list()
        if self.stop == "loadx":
            tb = []
        elif self.stop == "l0":
            tb = [b for b in tb if b[0] in ("pw1", "pw2")]
        elif self.stop == "ffn0":
            tb = [b for b in tb if b[0] in ("pw1", "pw2", "gu0", "dn0")]
        elif self.stop == "gdn":
            tb = [b for b in tb if b[0] not in ("gu1", "dn1")]
        for _ in range(ntiles):
            self.wq += tb
        self.wi = 0
        self.wl = 0
        if self.stop == "conv":
            fw.finish(self.b_conv + [self.b_cst, self.b_vecs, self.b_hrow, self.b_cin, self.b_cbf, self.b_sc])
            fw.es.close()
            return
        self.adaln()
        if self.stop == "adaln":
            fw.finish([self.b_coef])
            fw.es.close()
            return
        self.tiles = []
        for t in range(self.n_ptiles):
            self.tiles.append(dict(kind="p", t=t, T=TP, C=self.C, segs=[dict(seq=0, n=TP, off=0, first=(t == 0), last=(t == SEQ // TP - 1))]))
        if self.do_sample:
            self.tiles.append(dict(kind="s", t=0, T=NSMP * DSEQ, C=64,
                                   segs=[dict(seq=1 + s, n=DSEQ, off=s * DSEQ, first=True, last=True) for s in range(NSMP)]))
        self.prefetch_x(self.tiles[0])
        try:
            self.run_tiles()
        except StopBuild:
            pass
        fw.finish(self.outbufs)
        fw.es.close()

    def run_tiles(self):
        for i, tl in enumerate(self.tiles):
            self.load_x(tl)
            if self.stop != "loadx":
                self.layer0(tl)
            if i + 1 < len(self.tiles):
                self.prefetch_x(self.tiles[i + 1])
            if self.stop not in ("loadx", "l0"):
                self.ffn(tl, 0)
            if self.stop not in ("loadx", "l0", "ffn0"):
                self.gdn(tl)
            if self.stop not in ("loadx", "l0", "ffn0", "gdn"):
                self.ffn(tl, 1)
            self.store_y(tl)

    def load_consts(self):
        fw, nc = self.fw, self.nc
        fw.dma(fw.sp, self.cst[:], self.consts_d[:], self.d_misc[0], writes=[self.b_cst])
        fw.dma(fw.sp, self.vecs[:], self.vecs_d[:], self.d_misc[1], writes=[self.b_vecs])
        fw.dma(fw.sp, self.hrow[:], self.hrow_d[0:1, :].broadcast_to([128, 32]), self.d_misc[2], writes=[self.b_hrow])
        fw.dma(fw.sp, self.c5s[0:1 + NSMP, :], self.c5[:], self.d_misc[3], writes=[self.b_cin])
        fw.op(fw.dve, lambda: nc.vector.tensor_copy(self.cbf[:], self.cst[:, 0:384]), reads=[self.b_cst], writes=[self.b_cbf])
        nm = 2 * 7 * 128
        tv = self.tmp[:].rearrange("p a b -> p (a b)")[:, 0:nm]
        fw.dma(fw.sp, tv, self.masks_d[:], self.d_misc[4], writes=self.b_tmp)
        fw.op(fw.dve, lambda: nc.vector.tensor_copy(self.masks[:], tv), reads=self.b_tmp, writes=[self.b_masks])
        fw.op(fw.dve, lambda: nc.vector.memset(self.sc[:, 0:1], EPS), writes=[self.b_sc])
        fw.op(fw.dve, lambda: nc.vector.memset(self.sc[:, 1:2], 1.0), writes=[self.b_sc])
        fw.op(fw.dve, lambda: nc.vector.memset(self.sc[:, 2:3], 0.0), writes=[self.b_sc])
        fw.op(fw.act, lambda: nc.scalar.activation(self.hrow[:, 0:16], self.hrow[:, 0:16], AF.Exp), reads=[self.b_hrow], writes=[self.b_hrow])
        fw.op(fw.dve, lambda: nc.vector.tensor_scalar(self.hrow[:, 0:16], self.hrow[:, 0:16], -1.0, None, ALU.mult),
              reads=[self.b_hrow], writes=[self.b_hrow])
        for g in range(4):
            fw.op(fw.dve, lambda g=g: nc.vector.memset(self.S[:, 4 * g:4 * g + 4, :], 0.0), writes=[self.b_S[g]])
        fw.op(fw.dve, lambda: nc.vector.memset(self.ghalo[:, 0, :, :], 0.0), writes=self.b_ghalo[0])
        fw.op(fw.dve, lambda: nc.vector.memset(self.u[:, :, 0:CW - 1], 0.0), writes=self.b_u)

    @property
    def ident(self):
        return self.cst[:, 0:128]

    @property
    def ident_bf(self):
        return self.cbf[:, 0:128]

    @property
    def ones_bf(self):
        return self.cbf[:, 128:256]

    def convert_weights(self, total):
        fw = self.fw
        nch = total // CONV_CH
        self.b_conv = bufs(nch)
        self.d_conv = [fw.dsem("cv") for _ in range(nch)]
        for i in range(nch):
            src = bass.AP(self.wall, i * CONV_CH, [[8192, 128], [2048, 4], [1, 2048]])
            dst = bass.AP(self.wsc, i * CONV_CH, [[8192, 128], [2048, 4], [1, 2048]])
            if i >= 6:
                fw._wait(fw.pool, [self.b_conv[i - 6].w])
            fw.dma(fw.pool, dst, src, self.d_conv[i], writes=[self.b_conv[i]])

    def _issue_load(self, j):
        fw = self.fw
        name, idx, KC, ncol = self.wq[j]
        off = self.offs[(name, idx)]
        n = KC * ncol
        s = j % NSLOT
        c0 = off // CONV_CH
        c1 = (off + 128 * n - 1) // CONV_CH
        src = bass.AP(self.wsc, off, [[n, 128], [1, n]])
        fw.dma(fw.sp, self.wslot[s][:, 0:n], src, self.d_ws[s],
               reads=[self.b_conv[c] for c in range(c0, c1 + 1)], writes=[self.b_ws[s]])

    def wnext(self, expect):
        while self.wl < len(self.wq) and self.wl < self.wi + NSLOT:
            self._issue_load(self.wl)
            self.wl += 1
        name, idx, KC, ncol = self.wq[self.wi]
        assert name == expect, (name, expect)
        s = self.wi % NSLOT
        self.wi += 1
        return self.wslot[s][:, 0:KC * ncol].rearrange("p (k n) -> p k n", k=KC), self.b_ws[s]

    def adaln(self):
        fw, nc = self.fw, self.nc
        NS = 1 + NSMP
        fw.op(fw.act, lambda: nc.scalar.activation(self.c5s[0:NS, :], self.c5s[0:NS, :], AF.Silu), reads=[self.b_cin], writes=[self.b_cin])
        P = self.P[0]
        for c in range(8):
            fw.op(fw.pe, lambda c=c: nc.tensor.transpose(P[:, c * 8:c * 8 + NS], self.c5s[0:NS, c * 128:(c + 1) * 128], self.ident[0:NS, 0:NS]),
                  reads=[self.b_cin, self.b_cst], writes=[self.b_P[0]], inc=(c == 7))
        b_scT = Buf()
        fw.op(fw.dve, lambda: nc.vector.tensor_copy(self.scT[:, :, 0:NS], P[:, 0:64].rearrange("p (c s) -> p c s", c=8)[:, :, 0:NS]),
              reads=[self.b_P[0]], writes=[b_scT])
        Pm = self.P[1]
        for l in range(2):
            for b in range(12):
                w, bw = self.wnext("ada")
                for jj in range(4):
                    oc = b * 4 + jj
                    col = (l * 48 + oc) * NS
                    for k in range(8):
                        fw.op(fw.pe, lambda jj=jj, k=k, col=col, w=w: nc.tensor.matmul(
                            Pm[:, col:col + NS], w[:, k, jj * 128:(jj + 1) * 128], self.scT[:, k, 0:NS], start=(k == 0), stop=(k == 7)),
                            reads=[bw, b_scT], writes=[self.b_P[1]], inc=(k == 7 and jj == 3))
        b_mod = Buf()
        bada = self.V("b_ada", 0, 96)
        fw.op(fw.dve, lambda: nc.vector.tensor_tensor(out=self.modt[:, :, 0:NS], in0=Pm[:, 0:96 * NS].rearrange("p (a s) -> p a s", s=NS),
                                                      in1=bass.AP(self.vecs, VC["b_ada"], [[NVEC, 128], [1, 96], [0, NS]]), op=ALU.add),
              reads=[self.b_P[1], self.b_vecs], writes=[b_mod])
        for l in range(2):
            for kind, (m, gname) in enumerate([(1, "g_pre_mix"), (0, None), (2, "g_post_mix"), (4, "g_pre_ffn"), (3, None), (5, "g_post_ffn")]):
                dst = self.coef[:, (l * 6 + kind) * 40:(l * 6 + kind + 1) * 40].rearrange("p (c s) -> p c s", c=8)
                src = self.modt[:, l * 48 + m * 8:l * 48 + m * 8 + 8, 0:NS]
                if gname is None:
                    fw.op(fw.dve, lambda dst=dst, src=src: nc.vector.tensor_copy(dst, src), reads=[b_mod], writes=[self.b_coef])
                else:
                    gv = bass.AP(self.vecs, VC[gname] + 8 * l, [[NVEC, 128], [1, 8], [0, NS]])
                    if kind in (0, 3):
                        fw.op(fw.dve, lambda dst=dst, src=src, gv=gv: nc.vector.scalar_tensor_tensor(
                            out=dst, in0=src, scalar=1.0, in1=gv, op0=ALU.add, op1=ALU.mult), reads=[b_mod, self.b_vecs], writes=[self.b_coef])
                    else:
                        fw.op(fw.dve, lambda dst=dst, src=src, gv=gv: nc.vector.tensor_tensor(out=dst, in0=src, in1=gv, op=ALU.mult),
                              reads=[b_mod, self.b_vecs], writes=[self.b_coef])

    def cf(self, l, kind, c, seq):
        o = (l * 6 + kind) * 40 + c * 5 + seq
        return self.coef[:, o:o + 1]

    def _rows(self, tl, sub):
        if tl["kind"] == "p":
            r0 = tl["t"] * TP + sub * 128
            return self.xp[r0:r0 + 128, :], self.yp[r0:r0 + 128, :]
        r0 = sub * 128
        return self.xs[r0:r0 + 128, :], self.ys[r0:r0 + 128, :]

    def prefetch_x(self, tl, half=0):
        fw = self.fw
        for sub in range(2 * half, min(2 * half + 2, tl["T"] // 128)):
            src, _ = self._rows(tl, sub)
            fw.dma(fw.sp, self.xst[sub % 2][:], src, self.d_xst[sub % 2], writes=[self.b_xst[sub % 2]])

    def load_x(self, tl):
        fw, nc = self.fw, self.nc
        ns = tl["T"] // 128
        for half in range((ns + 1) // 2):
            if half == 1:
                self.prefetch_x(tl, 1)
            n2 = min(2, ns - 2 * half)
            for c in range(8):
                pb = c % 4
                P = self.P[pb]
                for s2 in range(n2):
                    fw.op(fw.pe, lambda c=c, s2=s2, P=P: nc.tensor.transpose(P[:, s2 * 128:(s2 + 1) * 128], self.xst[s2][:, c * 128:(c + 1) * 128], self.ident),
                          reads=[self.b_xst[s2], self.b_cst], writes=[self.b_P[pb]], inc=(s2 == n2 - 1))
                dst = self.xT[:, c, half * 256:half * 256 + n2 * 128]
                if c % 2 == 0:
                    fw.op(fw.dve, lambda dst=dst, P=P, n2=n2: nc.vector.tensor_copy(dst, P[:, 0:n2 * 128]), reads=[self.b_P[pb]], writes=[self.b_xT[c]])
                else:
                    fw.op(fw.act, lambda dst=dst, P=P, n2=n2: nc.scalar.copy(dst, P[:, 0:n2 * 128]), reads=[self.b_P[pb]], writes=[self.b_xT[c]])

    def store_y(self, tl):
        fw, nc = self.fw, self.nc
        ns = tl["T"] // 128
        tst = self.tmp[:].rearrange("p a b -> p (a b)")
        k = 0
        for sub in range(ns):
            for half in range(2):
                pb = k % 4
                k += 1
                P = self.P[pb]
                cidx = 2 * sub + half
                for cc in range(4):
                    c = half * 4 + cc
                    fw.op(fw.pe, lambda c=c, cc=cc, sub=sub, P=P: nc.tensor.transpose(P[:, cc * 128:(cc + 1) * 128], self.xT[:, c, sub * 128:(sub + 1) * 128], self.ident),
                          reads=[self.b_xT[c], self.b_cst], writes=[self.b_P[pb]], inc=(cc == 3))
                dstv = tst[:, cidx * 512:(cidx + 1) * 512]
                if half == 0:
                    fw.op(fw.dve, lambda P=P, dstv=dstv: nc.vector.tensor_copy(dstv, P[:, :]), reads=[self.b_P[pb]], writes=[self.b_tmp[cidx]])
                else:
                    fw.op(fw.act, lambda P=P, dstv=dstv: nc.scalar.copy(dstv, P[:, :]), reads=[self.b_P[pb]], writes=[self.b_tmp[cidx]])
            _, dst = self._rows(tl, sub)
            ob_out = Buf()
            fw.dma(fw.pool, dst, tst[:, sub * 1024:(sub + 1) * 1024], self.d_yst[sub], reads=[self.b_tmp[2 * sub], self.b_tmp[2 * sub + 1]], writes=[ob_out])
            self.outbufs.append(ob_out)

    def stats_sumsq(self, src_aps, src_bufs, T, pbank, nchunks=8):
        fw, nc = self.fw, self.nc
        P = self.P[pbank]
        for c in range(nchunks):
            if c % 2 == 0:
                fw.op(fw.act, lambda c=c: nc.scalar.activation(self.sq[:, c, 0:T], src_aps[:, c, :], AF.Square), reads=[src_bufs[c]], writes=[self.b_sq[c]])
            else:
                fw.op(fw.dve, lambda c=c: nc.vector.tensor_tensor(out=self.sq[:, c, 0:T], in0=src_aps[:, c, :], in1=src_aps[:, c, :], op=ALU.mult),
                      reads=[src_bufs[c]], writes=[self.b_sq[c]])
            fw.op(fw.pe, lambda c=c: nc.tensor.matmul(P[:, 0:T], self.ones_bf, self.sq[:, c, 0:T], start=(c == 0), stop=(c == nchunks - 1)),
                  reads=[self.b_sq[c], self.b_cbf], writes=[self.b_P[pbank]], inc=(c == nchunks - 1))

    def rstd_from(self, psrc_ap, psrc_bufs, T, scale, ri=0):
        fw, nc = self.fw, self.nc
        r = self.rs[:, ri, 0:T]
        fw.op(fw.act, lambda: nc.scalar.activation(r, psrc_ap, AF.Ln, bias=self.sc[:, 0:1], scale=scale), reads=list(psrc_bufs) + [self.b_sc], writes=[self.b_rs[ri]])
        fw.op(fw.act, lambda: nc.scalar.activation(r, r, AF.Exp, scale=-0.5), reads=[self.b_rs[ri]], writes=[self.b_rs[ri]])
        return r

    def bc8(self, t2d_tile, ri, T):
        return bass.AP(self.rs, ri * TP, [[2 * TP, 128], [0, 8], [1, T]])

    def norm_pre(self, tl, l, which):
        fw, nc = self.fw, self.nc
        T = tl["T"]
        kA, kB = (0, 1) if which == "mix" else (3, 4)
        self.stats_sumsq(self.xT[:, :, 0:T], self.b_xT, T, 4)
        self.rstd_from(self.P[4][:, 0:T], [self.b_P[4]], T, 1.0 / D)
        r0 = self.rs[:, 0, 0:T]
        for c in range(8):
            if c < 5:
                fw.op(fw.dve, lambda c=c: nc.vector.tensor_tensor(out=self.tmp[:, c, 0:T], in0=self.xT[:, c, 0:T], in1=r0, op=ALU.mult),
                      reads=[self.b_xT[c], self.b_rs[0]], writes=[self.b_tmp[c]])
            else:
                fw.op(fw.pool, lambda c=c: nc.gpsimd.tensor_tensor(out=self.tmp[:, c, 0:T], in0=self.xT[:, c, 0:T], in1=r0, op=ALU.mult),
                      reads=[self.b_xT[c], self.b_rs[0]], writes=[self.b_tmp[c]])
        for c in [0, 5, 1, 6, 2, 7, 3, 4]:
            for sg in tl["segs"]:
                o, n, s = sg["off"], sg["n"], sg["seq"]
                if c >= 5:
                    fw.op(fw.dve, lambda c=c, o=o, n=n, s=s: nc.vector.tensor_scalar(self.hT[:, c, o:o + n], self.tmp[:, c, o:o + n],
                                                                                    self.cf(l, kA, c, s), self.cf(l, kB, c, s), ALU.mult, ALU.add),
                          reads=[self.b_tmp[c], self.b_coef], writes=[self.b_hT[c]])
                else:
                    fw.op(fw.act, lambda c=c, o=o, n=n, s=s: nc.scalar.activation(self.hT[:, c, o:o + n], self.tmp[:, c, o:o + n], AF.Identity,
                                                                                 bias=self.cf(l, kB, c, s), scale=self.cf(l, kA, c, s)),
                          reads=[self.b_tmp[c], self.b_coef], writes=[self.b_hT[c]])

    def post_norm_residual(self, tl, l, which):
        fw, nc = self.fw, self.nc
        T = tl["T"]
        kG = 2 if which == "mix" else 5
        self.stats_sumsq(self.tmp[:, :, 0:T], self.b_tmp, T, 4)
        self.rstd_from(self.P[4][:, 0:T], [self.b_P[4]], T, 1.0 / D)
        r0 = self.rs[:, 0, 0:T]
        for c in range(8):
            if c < 3:
                fw.op(fw.dve, lambda c=c: nc.vector.tensor_tensor(out=self.tmp[:, c, 0:T], in0=self.tmp[:, c, 0:T], in1=r0, op=ALU.mult),
                      reads=[self.b_tmp[c], self.b_rs[0]], writes=[self.b_tmp[c]])
            else:
                fw.op(fw.pool, lambda c=c: nc.gpsimd.tensor_tensor(out=self.tmp[:, c, 0:T], in0=self.tmp[:, c, 0:T], in1=r0, op=ALU.mult),
                      reads=[self.b_tmp[c], self.b_rs[0]], writes=[self.b_tmp[c]])
        for c in range(8):
            for sg in tl["segs"]:
                o, n, s = sg["off"], sg["n"], sg["seq"]
                fw.op(fw.dve, lambda c=c, o=o, n=n, s=s: nc.vector.scalar_tensor_tensor(
                    out=self.xT[:, c, o:o + n], in0=self.tmp[:, c, o:o + n], scalar=self.cf(l, kG, c, s), in1=self.xT[:, c, o:o + n],
                    op0=ALU.mult, op1=ALU.add), reads=[self.b_tmp[c], self.b_coef], writes=[self.b_xT[c]])

    def dense(self, P, pb, w, bw, col0, rhs_fn, rhs_bufs, KC, T, last_inc=True):
        fw, nc = self.fw, self.nc
        for k in range(KC):
            fw.op(fw.pe, lambda k=k: nc.tensor.matmul(P[:, 0:T], w[:, k, col0:col0 + 128], rhs_fn(k), start=(k == 0), stop=(k == KC - 1)),
                  reads=[bw, rhs_bufs[k]], writes=[self.b_P[pb]], inc=(k == KC - 1 and last_inc))

    def useg(self, tl):
        return [i * (CW - 1 + sg["n"]) for i, sg in enumerate(tl["segs"])]

    def layer0(self, tl):
        fw, nc = self.fw, self.nc
        T = tl["T"]
        H = CW - 1
        uo = self.useg(tl)
        if tl["kind"] == "s":
            for i, sg in enumerate(tl["segs"]):
                fw.dma(fw.sp, self.cin[0:H, :], self.cconv[sg["seq"] - 1], self.d_cin, writes=[self.b_cin])
                P = self.P[i % 4]
                for c in range(8):
                    fw.op(fw.pe, lambda c=c, P=P: nc.tensor.transpose(P[:, c * 32:c * 32 + H], self.cin[0:H, c * 128:(c + 1) * 128], self.ident[0:H, 0:H]),
                          reads=[self.b_cin, self.b_cst], writes=[self.b_P[i % 4]], inc=(c == 7))
                fw.op(fw.dve, lambda P=P, i=i: nc.vector.tensor_copy(self.u[:, :, uo[i]:uo[i] + H], P[:, 0:256].rearrange("p (c h) -> p c h", c=8)[:, :, 0:H]),
                      reads=[self.b_P[i % 4]], writes=self.b_u)
        self.norm_pre(tl, 0, "mix")
        for b in range(4):
            w, bw = self.wnext("pw1")
            for jj in range(2):
                j = 2 * b + jj
                pa, pg = (0, 1) if j % 2 == 0 else (2, 3)
                self.dense(self.P[pa], pa, w, bw, jj * 128, lambda k: self.hT[:, k, 0:T], self.b_hT, 8, T)
                self.dense(self.P[pg], pg, w, bw, 256 + jj * 128, lambda k: self.hT[:, k, 0:T], self.b_hT, 8, T)
                sgi = j % 2
                fw.op(fw.act, lambda pg=pg, j=j, sgi=sgi: nc.scalar.activation(self.sg[sgi][:, 0:T], self.P[pg][:, 0:T], AF.Sigmoid, bias=self.V("b_pw1", 8 + j)),
                      reads=[self.b_P[pg], self.b_vecs], writes=[self.b_sg[sgi]])
                for i, sg in enumerate(tl["segs"]):
                    o, n = sg["off"], sg["n"]
                    fw.op(fw.dve, lambda pa=pa, j=j, sgi=sgi, o=o, n=n, i=i: nc.vector.scalar_tensor_tensor(
                        out=self.u[:, j, uo[i] + H:uo[i] + H + n], in0=self.P[pa][:, o:o + n], scalar=self.V("b_pw1", j), in1=self.sg[sgi][:, o:o + n],
                        op0=ALU.add, op1=ALU.mult), reads=[self.b_P[pa], self.b_sg[sgi], self.b_vecs], writes=[self.b_u[j]])
                    if sg["last"]:
                        fw.op(fw.dve, lambda pa=pa, j=j, sgi=sgi, o=o, n=n, i=i: nc.vector.scalar_tensor_tensor(
                            out=self.ucf[:, i, j, :], in0=self.P[pa][:, o + n - H:o + n], scalar=self.V("b_pw1", j), in1=self.sg[sgi][:, o + n - H:o + n],
                            op0=ALU.add, op1=ALU.mult), reads=[self.b_P[pa], self.b_sg[sgi], self.b_vecs], writes=[self.b_ucf[i]])
        for i, sg in enumerate(tl["segs"]):
            if sg["last"]:
                dst = self.o_ccp if tl["kind"] == "p" else self.o_ccs[sg["seq"] - 1]
                self.emit_rows_out(self.ucf[:, i], [self.b_ucf[i]], 8, H, dst)
        nseg = len(tl["segs"])
        nn = tl["segs"][0]["n"]
        UWt = self.u[:].ap[0][0]
        UWc = CW - 1 + TP
        for c in range(8):
            pb = c % 4
            for k in range(CW):
                slot = (c * CW + k) % NDG
                wv = self.V("w_dw", c * CW + k)
                fw.op(fw.dve, lambda slot=slot, wv=wv: nc.vector.tensor_scalar(self.dg[:, slot, :], self.ident_bf, wv, None, ALU.mult),
                      reads=[self.b_cbf, self.b_vecs], writes=[self.b_dg[slot]])
                rhs = bass.AP(self.u, c * UWc + k, [[UWt, 128], [CW - 1 + nn, nseg], [1, nn]])
                fw.op(fw.pe, lambda slot=slot, rhs=rhs, pb=pb, k=k: nc.tensor.matmul(self.P[pb][:, 0:T], self.dg[:, slot, :], rhs, start=(k == 0), stop=(k == CW - 1)),
                      reads=[self.b_dg[slot], self.b_u[c]], writes=[self.b_P[pb]], inc=True)
            fw.op(fw.act, lambda c=c, pb=pb: nc.scalar.activation(self.tmp[:, c, 0:T], self.P[pb][:, 0:T], AF.Identity, bias=self.V("b_dw", c)),
                  reads=[self.b_P[pb], self.b_vecs], writes=[self.b_tmp[c]])
        if tl["kind"] == "p":
            fw.op(fw.dve, lambda: nc.vector.tensor_copy(self.u[:, :, 0:H], self.u[:, :, T:T + H]), reads=self.b_u, writes=self.b_u)
        fw.op(fw.dve, lambda: nc.vector.tensor_copy(self.hT[:, :, 0:T], self.tmp[:, :, 0:T]), reads=self.b_tmp, writes=self.b_hT)
        for c in range(8):
            fw.op(fw.pe, lambda c=c: nc.tensor.matmul(self.P[5][:, 0:T], self.ones_bf, self.hT[:, c, 0:T], start=(c == 0), stop=(c == 7)),
                  reads=[self.b_hT[c], self.b_cbf], writes=[self.b_P[5]], inc=(c == 7))
        self.stats_sumsq(self.tmp[:, :, 0:T], self.b_tmp, T, 4)
        mu = self.rs[:, 1, 0:T]
        fw.op(fw.dve, lambda: nc.vector.tensor_scalar(mu, self.P[5][:, 0:T], 1.0 / D, None, ALU.mult), reads=[self.b_P[5]], writes=[self.b_rs[1]])
        v0 = self.sg[0][:, 0:T]
        fw.op(fw.dve, lambda: nc.vector.tensor_tensor(out=v0, in0=mu, in1=mu, op=ALU.mult), reads=[self.b_rs[1]], writes=[self.b_sg[0]])
        fw.op(fw.dve, lambda: nc.vector.scalar_tensor_tensor(out=v0, in0=self.P[4][:, 0:T], scalar=1.0 / D, in1=v0, op0=ALU.mult, op1=ALU.subtract),
              reads=[self.b_P[4], self.b_sg[0]], writes=[self.b_sg[0]])
        self.rstd_from(v0, [self.b_sg[0]], T, 1.0)
        fw.op(fw.dve, lambda: nc.vector.tensor_tensor(out=self.tmp[:, :, 0:T], in0=self.tmp[:, :, 0:T], in1=self.bc8(self.rs, 1, T), op=ALU.subtract),
              reads=self.b_tmp + [self.b_rs[1]], writes=self.b_tmp)
        fw.op(fw.dve, lambda: nc.vector.tensor_tensor(out=self.tmp[:, :, 0:T], in0=self.tmp[:, :, 0:T], in1=self.bc8(self.rs, 0, T), op=ALU.mult),
              reads=self.b_tmp + [self.b_rs[0]], writes=self.b_tmp)
        for c in range(8):
            fw.op(fw.act, lambda c=c: nc.scalar.activation(self.hT[:, c, 0:T], self.tmp[:, c, 0:T], AF.Silu, bias=self.V("ln_b", c), scale=self.V("ln_g", c)),
                  reads=[self.b_tmp[c], self.b_vecs], writes=[self.b_hT[c]])
        for b in range(2):
            w, bw = self.wnext("pw2")
            for jj in range(4):
                j = 4 * b + jj
                pb = j % 4
                self.dense(self.P[pb], pb, w, bw, jj * 128, lambda k: self.hT[:, k, 0:T], self.b_hT, 8, T)
                fw.op(fw.act, lambda j=j, pb=pb: nc.scalar.activation(self.tmp[:, j, 0:T], self.P[pb][:, 0:T], AF.Identity, bias=self.V("b_pw2", j)),
                      reads=[self.b_P[pb], self.b_vecs], writes=[self.b_tmp[j]])
        self.post_norm_residual(tl, 0, "mix")

    def emit_rows_out(self, src_tile, src_bufs, nchunk, nrow, dst_ap):
        fw, nc = self.fw, self.nc
        for c0 in range(0, nchunk, 4):
            pb = 6
            P = self.P[pb]
            nn = min(4, nchunk - c0)
            for cc in range(nn):
                fw.op(fw.pe, lambda cc=cc, c0=c0: nc.tensor.transpose(P[0:nrow, cc * 128:(cc + 1) * 128], src_tile[:, c0 + cc, 0:nrow], self.ident),
                      reads=list(src_bufs) + [self.b_cst], writes=[self.b_P[pb]], inc=(cc == nn - 1))
            g0 = (c0 // 8) * 8
            fw.op(fw.act, lambda c0=c0, nn=nn, g0=g0: nc.scalar.copy(self.cin[0:nrow, (c0 - g0) * 128:(c0 - g0 + nn) * 128], P[0:nrow, 0:nn * 128]),
                  reads=[self.b_P[pb]], writes=[self.b_cin])
            if c0 + nn >= min(nchunk, g0 + 8):
                ob = Buf()
                fw.dma(fw.pool, dst_ap[:, g0 * 128:(c0 + nn) * 128], self.cin[0:nrow, 0:(c0 + nn - g0) * 128], self.d_cin, reads=[self.b_cin], writes=[ob])
                self.outbufs.append(ob)

    def ffn(self, tl, l):
        fw, nc = self.fw, self.nc
        T = tl["T"]
        self.norm_pre(tl, l, "ffn")
        for b in range(11):
            w, bw = self.wnext(f"gu{l}")
            for jj in range(2):
                j = 2 * b + jj
                pa, pg = (0, 1) if j % 2 == 0 else (2, 3)
                self.dense(self.P[pa], pa, w, bw, jj * 128, lambda k: self.hT[:, k, 0:T], self.b_hT, 8, T)
                self.dense(self.P[pg], pg, w, bw, 256 + jj * 128, lambda k: self.hT[:, k, 0:T], self.b_hT, 8, T)
                sgi = j % 2
                fw.op(fw.act, lambda pa=pa, sgi=sgi: nc.scalar.activation(self.sg[sgi][:, 0:T], self.P[pa][:, 0:T], AF.Silu),
                      reads=[self.b_P[pa]], writes=[self.b_sg[sgi]])
                fw.op(fw.dve, lambda pg=pg, sgi=sgi, j=j: nc.vector.tensor_tensor(out=self.CH[:, j, 0:T], in0=self.sg[sgi][:, 0:T], in1=self.P[pg][:, 0:T], op=ALU.mult),
                      reads=[self.b_P[pg], self.b_sg[sgi]], writes=[self.b_CH[j]])
        for b in range(8):
            w, bw = self.wnext(f"dn{l}")
            pb = b % 4
            self.dense(self.P[pb], pb, w, bw, 0, lambda k: self.CH[:, k, 0:T], self.b_CH, 22, T)
            if b % 2 == 0:
                fw.op(fw.act, lambda b=b, pb=pb: nc.scalar.copy(self.tmp[:, b, 0:T], self.P[pb][:, 0:T]), reads=[self.b_P[pb]], writes=[self.b_tmp[b]])
            else:
                fw.op(fw.dve, lambda b=b, pb=pb: nc.vector.tensor_copy(self.tmp[:, b, 0:T], self.P[pb][:, 0:T]), reads=[self.b_P[pb]], writes=[self.b_tmp[b]])
        self.post_norm_residual(tl, l, "ffn")

    def gdn_alloc(self):
        fw = self.fw
        sb = fw.sb
        G = 4
        CA = self.CA = 128
        LA = self.LA = 7
        self.ab = sb("ab", [128, 8, 32], F32)
        self.gt = sb("gt", [128, 8, 16], F32)
        self.beta = sb("beta", [128, 8, 16], F32)
        self.gc = sb("gc", [128, 8, 16], F32)
        self.egc = sb("egc", [128, 8, 16], F32)
        self.t16 = [sb(f"t16{i}", [128, 8, 16], F32) for i in range(3)]
        self.g3 = sb("g3", [128, 3, 8, 16], BF16)
        fl = lambda n, k, dt: sb(n, [128, k], dt)
        self.QKDs = [fl(f"QKD{i}", G * CA, BF16) for i in range(4)]
        self.ATs = [fl(f"AT{i}", G * CA, BF16) for i in range(4)]
        self.dlasts = [sb(f"dlast{i}", [128, G], F32) for i in range(4)]
        self.gtots = [sb(f"gtot{i}", [128, G], F32) for i in range(4)]
        self.masks = fl("masks", 2 * LA * CA, BF16)
        self.b_masks = Buf()
        self.pa = []
        for si in range(2):
            d = {}
            bb = {}
            def mk(name, n, dt, alias=None, abuf=None):
                if alias is not None:
                    d[name] = alias
                    bb[name] = abuf
                else:
                    d[name] = fl(f"{name}{si}", n, dt)
                    bb[name] = Buf(f"{name}{si}")
            mk("rhsU", 3 * G * CA, BF16)
            if si == 0:
                mk("rhsB", G * CA, BF16)
                mk("DT", G * CA, F32)
                mk("Bm", G * CA, F32)
                mk("X0", G * CA, BF16)
            else:
                mk("rhsB", 0, BF16, alias=self.sq1[0], abuf=self.b_sq1[0])
                mk("DT", 0, F32, alias=self.qkf[0], abuf=self.b_qkf[0])
                mk("Bm", 0, F32, alias=self.qkf[1], abuf=self.b_qkf[1])
                mk("X0", 0, BF16, alias=self.sq1[1], abuf=self.b_sq1[1])
            mk("Y0", G * CA, BF16)
            for nm in ("A0", "A1", "B0", "B1", "Q0", "Q1", "No0", "No1", "Mo0", "Mo1"):
                mk(nm, G * CA, BF16)
            self.pa.append((d, bb))
        self.Sbf = fl("Sbf", G * HD, BF16)
        self.rr = fl("rr", G * HD, BF16)
        self.vn = fl("vn", G * HD, BF16)
        self.vnd = fl("vnd", G * HD, BF16)
        self.oo = fl("oo", G * HD, F32)
        self.osq = fl("osq", G * HD, F32)
        self.t1 = self.oo
        self.t2 = self.osq
        self.oss = sb("oss", [128, 2, G], F32)
        self.ktok = fl("ktok", 2 * HD, BF16)
        n = ["ab", "gt", "beta", "gc", "egc", "t160", "t161", "t162", "g3",
             "Sbf", "AT0", "AT1", "AT2", "AT3", "QKD0", "QKD1", "QKD2", "QKD3", "dl0", "dl1", "dl2", "dl3", "rr", "vn", "vnd", "oo", "osq", "oss", "ktok"]
        self.gb = {k: Buf(k) for k in n}
        self.gb["t1"] = self.gb["oo"]
        self.gb["t2"] = self.gb["osq"]

    def gdn(self, tl):
        fw, nc = self.fw, self.nc
        T = tl["T"]
        C = tl["C"]
        G = 4
        gb = self.gb
        self.norm_pre(tl, 1, "mix")
        nch = T // C
        chunks = []
        for sg in tl["segs"]:
            for q in range(sg["n"] // C):
                chunks.append((sg["off"] + q * C, sg, q == 0, q == sg["n"] // C - 1))
        w, bw = self.wnext("gab")
        Pab = self.P[6]
        for n_, (co, sg, _, _) in enumerate(chunks):
            for k in range(8):
                fw.op(fw.pe, lambda n_=n_, k=k, co=co: nc.tensor.matmul(Pab[0:C, n_ * 32:(n_ + 1) * 32], self.hT[:, k, co:co + C], w[:, k, 0:32], start=(k == 0), stop=(k == 7)),
                      reads=[bw, self.b_hT[k]], writes=[self.b_P[6]], inc=(k == 7 and n_ == nch - 1))
        abv = Pab[0:C, 0:nch * 32].rearrange("p (n j) -> p n j", j=32)
        A16 = lambda t: t[0:C, 0:nch, :]
        fw.op(fw.act, lambda: nc.scalar.activation(A16(self.beta), abv[:, :, 16:32], AF.Sigmoid), reads=[self.b_P[6]], writes=[gb["beta"]])
        xx, ax, ee = A16(self.t16[0]), A16(self.t16[1]), A16(self.t16[2])
        dtb = bass.AP(self.hrow, 16, [[32, C], [0, nch], [1, 16]])
        nA = bass.AP(self.hrow, 0, [[32, C], [0, nch], [1, 16]])
        fw.op(fw.dve, lambda: nc.vector.tensor_tensor(out=xx, in0=abv[:, :, 0:16], in1=dtb, op=ALU.add), reads=[self.b_P[6], self.b_hrow], writes=[gb["t160"]])
        fw.op(fw.act, lambda: nc.scalar.activation(ax, xx, AF.Abs), reads=[gb["t160"]], writes=[gb["t161"]])
        fw.op(fw.act, lambda: nc.scalar.activation(ee, ax, AF.Exp, scale=-1.0), reads=[gb["t161"]], writes=[gb["t162"]])
        fw.op(fw.act, lambda: nc.scalar.activation(ee, ee, AF.Ln, bias=self.sc[0:C, 1:2]), reads=[gb["t162"], self.b_sc], writes=[gb["t162"]])
        fw.op(fw.dve, lambda: nc.vector.scalar_tensor_tensor(out=xx, in0=xx, scalar=0.0, in1=ee, op0=ALU.max, op1=ALU.add),
              reads=[gb["t160"], gb["t162"]], writes=[gb["t160"]])
        fw.op(fw.dve, lambda: nc.vector.tensor_tensor(out=A16(self.gt), in0=xx, in1=nA, op=ALU.mult), reads=[gb["t160"], self.b_hrow], writes=[gb["gt"]])
        Pg = self.P[5]
        Ub = self.cbf[0:C, 256:256 + C]
        G3 = lambda k: self.g3[0:C, k, 0:nch, :]
        r1, r2 = A16(self.t16[1]), A16(self.t16[2])
        fw.op(fw.dve, lambda: nc.vector.tensor_copy(G3(0), A16(self.gt)), reads=[gb["gt"]], writes=[gb["g3"]])
        fw.op(fw.dve, lambda: nc.vector.tensor_tensor(out=r1, in0=A16(self.gt), in1=G3(0), op=ALU.subtract), reads=[gb["gt"], gb["g3"]], writes=[gb["t161"]])
        fw.op(fw.dve, lambda: nc.vector.tensor_copy(G3(1), r1), reads=[gb["t161"]], writes=[gb["g3"]])
        fw.op(fw.dve, lambda: nc.vector.tensor_tensor(out=r2, in0=r1, in1=G3(1), op=ALU.subtract), reads=[gb["t161"], gb["g3"]], writes=[gb["t162"]])
        fw.op(fw.dve, lambda: nc.vector.tensor_copy(G3(2), r2), reads=[gb["t162"]], writes=[gb["g3"]])
        for n_ in range(nch):
            for k in range(3):
                fw.op(fw.pe, lambda n_=n_, k=k: nc.tensor.matmul(Pg[0:C, n_ * 16:(n_ + 1) * 16], Ub, self.g3[0:C, k, n_, :], start=(k == 0), stop=(k == 2)),
                      reads=[gb["g3"], self.b_cbf], writes=[self.b_P[5]], inc=(n_ == nch - 1 and k == 2))
        fw.op(fw.dve, lambda: nc.vector.tensor_copy(A16(self.gc), Pg[0:C, 0:nch * 16].rearrange("p (n j) -> p n j", j=16)), reads=[self.b_P[5]], writes=[gb["gc"]])
        fw.op(fw.act, lambda: nc.scalar.activation(A16(self.egc), A16(self.gc), AF.Exp), reads=[gb["gc"]], writes=[gb["egc"]])

        if self.stop == "gdn_ab":
            raise StopBuild()
        for g in range(G):
            self.gdn_group(tl, g, chunks)
        for b2 in range(4):
            w, bw = self.wnext("gout")
            for jj in range(2):
                b = 2 * b2 + jj
                pb = b % 4
                self.dense(self.P[pb], pb, w, bw, jj * 128, lambda k: self.CH[:, k, 0:T], self.b_CH, 16, T)
                if b % 2 == 0:
                    fw.op(fw.act, lambda b=b, pb=pb: nc.scalar.copy(self.tmp[:, b, 0:T], self.P[pb][:, 0:T]), reads=[self.b_P[pb]], writes=[self.b_tmp[b]])
                else:
                    fw.op(fw.dve, lambda b=b, pb=pb: nc.vector.tensor_copy(self.tmp[:, b, 0:T], self.P[pb][:, 0:T]), reads=[self.b_P[pb]], writes=[self.b_tmp[b]])
        self.post_norm_residual(tl, 1, "mix")
        for sg in tl["segs"]:
            if sg["last"]:
                hs = sg["seq"]
                dst = self.o_gcp if tl["kind"] == "p" else self.o_gcs[hs - 1]
                self.emit_rows_out(self.ghalo[:, hs], self.b_ghalo[hs], 32, SCW - 1, dst)

    def gdn_group(self, tl, g, chunks):
        fw, nc = self.fw, self.nc
        T = tl["T"]
        C = tl["C"]
        G = 4
        gb = self.gb
        HL = SCW - 1
        QC, KC_, VC_, ZC = 16, 18, 20, 24
        segs = tl["segs"]
        so = [i * (HL + sg["n"]) for i, sg in enumerate(segs)]
        cnt = 0
        for bi in range(3):
            w, bw = self.wnext("gin")
            pend = []
            for jj in range(4):
                pb = cnt % 4
                cnt += 1
                P = self.P[pb]
                self.dense(P, pb, w, bw, jj * 128, lambda k: self.hT[:, k, 0:T], self.b_hT, 8, T)
                if bi == 2:
                    dch = ZC + jj
                    fw.op(fw.act, lambda P=P, dch=dch: nc.scalar.activation(self.CH[:, dch, 0:T], P[:, 0:T], AF.Silu), reads=[self.b_P[pb]], writes=[self.b_CH[dch]])
                    continue
                if bi == 0:
                    cch = (2 * g + jj) if jj < 2 else (8 + 2 * g + jj - 2)
                    dch = (QC + jj) if jj < 2 else (KC_ + jj - 2)
                else:
                    cch = 16 + 4 * g + jj
                    dch = VC_ + jj
                si = cnt % 2
                bst = self.b_cstg[si]
                stg = self.cstg[si][:].bitcast(BF16)
                pst_stg = stg.ap[0][0]
                pcv = 5 + si
                Pc = self.P[pcv]
                nseg = len(segs)
                nn = segs[0]["n"]
                for i, sg in enumerate(segs):
                    hs = sg["seq"]
                    o, n = sg["off"], sg["n"]
                    if tl["kind"] == "s" and g == 0 and bi == 0 and jj == 0:
                        self.load_ghalo(sg)
                    fw.op(fw.act, lambda P=P, stg=stg, i=i, o=o, n=n: nc.scalar.copy(stg[:, so[i] + HL:so[i] + HL + n], P[:, o:o + n]),
                          reads=[self.b_P[pb]], writes=[bst])
                    fw.op(fw.dve, lambda stg=stg, i=i, hs=hs, cch=cch: nc.vector.tensor_copy(stg[:, so[i]:so[i] + HL], self.ghalo[:, hs, cch, :]),
                          reads=[self.b_ghalo[hs][cch]], writes=[bst])
                    fw.op(fw.dve, lambda P=P, hs=hs, cch=cch, o=o, n=n: nc.vector.tensor_copy(self.ghalo[:, hs, cch, :], P[:, o + n - HL:o + n]),
                          reads=[self.b_P[pb]], writes=[self.b_ghalo[hs][cch]])
                dgv = self.sg[1][:].bitcast(BF16).rearrange("p (s c) -> p s c", s=8)
                for k in range(SCW):
                    slot = (self.dg2i) % 8
                    self.dg2i += 1
                    wv = self.V("w_gc", cch * SCW + k)
                    fw.op(fw.dve, lambda slot=slot, wv=wv: nc.vector.tensor_scalar(dgv[:, slot, :], self.ident_bf, wv, None, ALU.mult),
                          reads=[self.b_cbf, self.b_vecs], writes=[self.b_dg2[slot]])
                    rhs = stg[:, 0:nseg * (HL + nn)].rearrange("p (s w) -> p s w", s=nseg)[:, :, k:k + nn]
                    fw.op(fw.pe, lambda slot=slot, rhs=rhs, k=k: nc.tensor.matmul(Pc[:, 0:T], dgv[:, slot, :], rhs, start=(k == 0), stop=(k == SCW - 1)),
                          reads=[self.b_dg2[slot], bst], writes=[self.b_P[pcv]], inc=True)
                cvo, bcv = Pc, self.b_P[pcv]
                if bi == 1:
                    fw.op(fw.act, lambda cvo=cvo, dch=dch: nc.scalar.activation(self.CH[:, dch, 0:T], cvo[:, 0:T], AF.Silu), reads=[bcv], writes=[self.b_CH[dch]])
                else:
                    qf, bqf = [(self.qkf[0][:, 0:T], self.b_qkf[0]), (self.qkf[1][:, 0:T], self.b_qkf[1]),
                               (self.sg[0][:, 0:T], self.b_sg[0]), (self.rs[:, 1, 0:T], self.b_rs[1])][jj]
                    fw.op(fw.act, lambda cvo=cvo, qf=qf: nc.scalar.activation(qf, cvo[:, 0:T], AF.Silu), reads=[bcv], writes=[bqf])
                    pend.append((qf, bqf, dch, jj, si))
            for (qf, bqf, dch, jj, si) in (pend if bi == 0 else []):
                    s1, bs1 = self.sq1[si], self.b_sq1[si]
                    fw.op(fw.act, lambda qf=qf, s1=s1: nc.scalar.activation(s1[:, 0:T], qf, AF.Square), reads=[bqf], writes=[bs1])
                    fw.op(fw.pe, lambda s1=s1: nc.tensor.matmul(self.P[4][:, 0:T], self.ones_bf, s1[:, 0:T], start=True, stop=True),
                          reads=[bs1, self.b_cbf], writes=[self.b_P[4]])
                    r = self.rstd_from(self.P[4][:, 0:T], [self.b_P[4]], T, 1.0)
                    if jj < 2:
                        fw.op(fw.dve, lambda qf=qf, dch=dch, r=r: nc.vector.scalar_tensor_tensor(
                            out=self.CH[:, dch, 0:T], in0=qf, scalar=float(HD ** -0.5), in1=r, op0=ALU.mult, op1=ALU.mult),
                            reads=[bqf, self.b_rs[0]], writes=[self.b_CH[dch]])
                    else:
                        fw.op(fw.dve, lambda qf=qf, dch=dch, r=r: nc.vector.tensor_tensor(out=self.CH[:, dch, 0:T], in0=qf, in1=r, op=ALU.mult),
                              reads=[bqf, self.b_rs[0]], writes=[self.b_CH[dch]])
        if self.stop == "gdn_proj":
            raise StopBuild()
        def run(gens):
            gens = list(gens)
            while gens:
                for gen in list(gens):
                    try:
                        next(gen)
                    except StopIteration:
                        gens.remove(gen)
        N = len(chunks)
        act_gens = {}
        doneA, doneB = set(), set()
        nextA, nextB = 0, 0
        while len(doneB) < N:
            while nextA < N and sum(1 for k in act_gens if k[0] == "A") < NA_INFLIGHT and (nextA < 4 or (nextA - 4) in doneB):
                act_gens[("A", nextA)] = self.gdn_phaseA(tl, g, nextA, chunks[nextA])
                nextA += 1
            if nextB < N and nextB in doneA and not any(k[0] == "B" for k in act_gens):
                act_gens[("B", nextB)] = self.gdn_phaseB(tl, g, nextB, chunks[nextB])
                nextB += 1
            for key in list(act_gens):
                try:
                    if SEQ_EMIT:
                        for _ in act_gens[key]:
                            pass
                        raise StopIteration
                    next(act_gens[key])
                except StopIteration:
                    del act_gens[key]
                    (doneA if key[0] == "A" else doneB).add(key[1])

    def _gviews(self, tl, g, n_):
        C = tl["C"]
        G = 4
        v = dict(
            gv=lambda t, off=0: t[0:C, off:off + G * C].rearrange("p (g c) -> p g c", g=G),
            g2=lambda t, off=0: t[0:C, off:off + G * C],
            hv=lambda t: t[0:C, 0:G * HD].rearrange("p (g v) -> p g v", g=G),
            h2=lambda t: t[0:C, 0:G * HD],
            hsl=lambda t, hh, w=None: t[0:C, hh * (w or C):(hh + 1) * (w or C)],
            bc_mask=lambda base: bass.AP(self.cst, base, [[640, C], [0, G], [1, C]]),
            colb=lambda t, m: bass.AP(t, n_ * 16 + 4 * g, [[128, C], [1, G], [0, m]]),
            psv=lambda Pt, w=None: Pt[0:C, 0:G * (w or C)].rearrange("p (g c) -> p g c", g=G),
        )
        return v

    def gdn_phaseA(self, tl, g, n_, chunk):
        fw, nc = self.fw, self.nc
        co, sg, cfirst, clast = chunk
        C = tl["C"]
        G = 4
        CA, LA = self.CA, self.LA
        L = int(np.log2(C))
        gb = self.gb
        P, bP = self.P, self.b_P
        T_, Bf = self.pa[n_ % 2]
        par = n_ % 4
        pa, pb_ = (2, 6) if n_ % 2 == 0 else (0, 1)
        pc = pa
        V = self._gviews(tl, g, n_)
        gv, g2, hsl, bc_mask, colb, psv = V["gv"], V["g2"], V["hsl"], V["bc_mask"], V["colb"], V["psv"]
        ATp, QKDp = self.ATs[par], self.QKDs[par]
        bATp, bQKDp, bdl = gb[f"AT{par}"], gb[f"QKD{par}"], gb[f"dl{par}"]
        dlast, gtot = self.dlasts[par], self.gtots[par]
        rhsU, rhsB, DT, Bm, X0, Y0 = T_["rhsU"], T_["rhsB"], T_["DT"], T_["Bm"], T_["X0"], T_["Y0"]
        for k in range(3):
            g3b = bass.AP(self.g3, k * 128 + n_ * 16 + 4 * g, [[384, C], [1, G], [0, C]])
            fw.op(fw.dve, lambda k=k, g3b=g3b: nc.vector.tensor_tensor(out=gv(rhsU, k * G * C), in0=bass.AP(self.cbf, 256, [[384, C], [0, G], [1, C]]), in1=g3b, op=ALU.mult),
                  reads=[self.b_cbf, gb["g3"]], writes=[Bf["rhsU"]])
        fw.op(fw.dve, lambda: nc.vector.tensor_tensor(out=gv(rhsB), in0=bc_mask(0), in1=colb(self.beta, C), op=ALU.mult),
              reads=[self.b_cst, gb["beta"]], writes=[Bf["rhsB"]])
        yield
        for k in range(3):
            fw.op(fw.pe, lambda k=k: nc.tensor.matmul(P[pa][:, 0:G * C], self.cbf[0:C, 128:256], g2(rhsU, k * G * C), start=(k == 0), stop=(k == 2)),
                  reads=[Bf["rhsU"], self.b_cbf], writes=[bP[pa]], inc=(k == 2))
        fw.op(fw.pe, lambda: nc.tensor.matmul(P[pb_][0:C, 0:G * C], self.cbf[0:C, 128:128 + C], g2(rhsB), start=True, stop=True),
              reads=[Bf["rhsB"], self.b_cbf], writes=[bP[pb_]])
        yield
        gcrow = P[pa][:, 0:G * C].rearrange("p (g c) -> p g c", g=G)
        brow = psv(P[pb_])
        fw.op(fw.dve, lambda: nc.vector.tensor_tensor(out=gv(DT), in0=gcrow[0:C], in1=colb(self.gc, C), op=ALU.subtract),
              reads=[bP[pa], gb["gc"]], writes=[Bf["DT"]])
        fw.op(fw.act, lambda: nc.scalar.activation(dlast[0:C, :], gv(DT)[:, :, C - 1], AF.Exp), reads=[Bf["DT"]], writes=[bdl])
        fw.op(fw.act, lambda: nc.scalar.activation(gtot[:, :], gcrow[:, :, C - 1], AF.Exp), reads=[bP[pa]], writes=[bdl])
        fw.op(fw.dve, lambda: nc.vector.scalar_tensor_tensor(out=gv(DT), in0=gv(DT), scalar=0.0, in1=bc_mask(384), op0=ALU.min, op1=ALU.add),
              reads=[Bf["DT"], self.b_cst], writes=[Bf["DT"]])
        yield
        for jj in range(2):
            kT = self.CH[:, 18 + jj, co:co + C]
            qT = self.CH[:, 16 + jj, co:co + C]
            fw.op(fw.pe, lambda jj=jj, kT=kT: nc.tensor.matmul(P[pc][0:C, (jj * 2) * C:(jj * 2 + 1) * C], kT, kT, start=True, stop=True),
                  reads=[self.b_CH[18 + jj]], writes=[bP[pc]], inc=False)
            fw.op(fw.pe, lambda jj=jj, kT=kT, qT=qT: nc.tensor.matmul(P[pc][0:C, (jj * 2 + 1) * C:(jj * 2 + 2) * C], kT, qT, start=True, stop=True),
                  reads=[self.b_CH[18 + jj], self.b_CH[16 + jj]], writes=[bP[pc]], inc=(jj == 1))
        fw.op(fw.act, lambda: nc.scalar.activation(g2(DT), g2(DT), AF.Exp), reads=[Bf["DT"]], writes=[Bf["DT"]])
        yield
        kk_b = bass.AP(P[pc], 0, [[512, C], [2 * C, 2], [0, 2], [1, C]])
        kq_b = bass.AP(P[pc], C, [[512, C], [2 * C, 2], [0, 2], [1, C]])
        v4 = lambda t: t[0:C, 0:G * C].rearrange("p (a e c) -> p a e c", a=2, e=2)
        fw.op(fw.dve, lambda: nc.vector.tensor_tensor(out=v4(QKDp), in0=kq_b, in1=v4(DT), op=ALU.mult), reads=[bP[pc], Bf["DT"]], writes=[bQKDp])
        fw.op(fw.dve, lambda: nc.vector.tensor_tensor(out=gv(DT), in0=gv(DT), in1=bc_mask(512), op=ALU.mult), reads=[Bf["DT"], self.b_cst], writes=[Bf["DT"]])
        yield
        fw.op(fw.dve, lambda: nc.vector.tensor_tensor(out=gv(Bm), in0=brow, in1=gv(DT), op=ALU.mult), reads=[bP[pb_], Bf["DT"]], writes=[Bf["Bm"]])
        for a in range(2):
            kka = bass.AP(P[pc], a * 2 * C, [[512, C], [0, 2], [1, C]])
            sl = lambda t, a=a: t[0:C, a * 2 * C:(a + 1) * 2 * C].rearrange("p (e c) -> p e c", e=2)
            fw.op(fw.dve, lambda kka=kka, sl=sl: nc.vector.scalar_tensor_tensor(out=sl(X0), in0=kka, scalar=-1.0, in1=sl(Bm), op0=ALU.mult, op1=ALU.mult),
                  reads=[bP[pc], Bf["Bm"]], writes=[Bf["X0"]])
        for hh in range(G):
            fw.op(fw.pe, lambda hh=hh: nc.tensor.transpose(self.PB[0:C, hh * C:(hh + 1) * C], hsl(X0, hh), self.cbf[0:C, 0:C]),
                  reads=[Bf["X0"], self.b_cbf], writes=[self.b_PB], inc=(hh == G - 1))
        mkl = lambda which, lv: bass.AP(self.masks, which * LA * CA + lv * CA, [[2 * LA * CA, C], [0, G], [1, C]])
        idb = bass.AP(self.cbf, 0, [[384, C], [0, G], [1, C]])
        A = [T_["A0"], T_["A1"]]
        B = [T_["B0"], T_["B1"]]
        Q = [T_["Q0"], T_["Q1"]]
        No = [T_["No0"], T_["No1"]]
        Mo = [T_["Mo0"], T_["Mo1"]]
        nA, nB, nQ, nNo, nMo = ["A0", "A1"], ["B0", "B1"], ["Q0", "Q1"], ["No0", "No1"], ["Mo0", "Mo1"]
        fw.op(fw.act, lambda: nc.scalar.copy(g2(Y0), self.PB[0:C, 0:G * C]), reads=[self.b_PB], writes=[Bf["Y0"]])
        yield
        fw.op(fw.dve, lambda: nc.vector.tensor_tensor(out=gv(No[0]), in0=gv(X0), in1=mkl(0, 0), op=ALU.mult), reads=[Bf["X0"], self.b_masks], writes=[Bf["No0"]])
        fw.op(fw.dve, lambda: nc.vector.tensor_tensor(out=gv(B[1]), in0=gv(No[0]), in1=idb, op=ALU.add), reads=[Bf["No0"], self.b_cbf], writes=[Bf["B1"]])
        yield
        fw.op(fw.dve, lambda: nc.vector.tensor_tensor(out=gv(Mo[0]), in0=gv(Y0), in1=mkl(1, 0), op=ALU.mult), reads=[Bf["Y0"], self.b_masks], writes=[Bf["Mo0"]])
        fw.op(fw.dve, lambda: nc.vector.tensor_tensor(out=gv(A[1]), in0=gv(Mo[0]), in1=idb, op=ALU.add), reads=[Bf["Mo0"], self.b_cbf], writes=[Bf["A1"]])
        fw.op(fw.dve, lambda: nc.vector.tensor_tensor(out=gv(No[1]), in0=gv(X0), in1=mkl(0, 1), op=ALU.mult), reads=[Bf["X0"], self.b_masks], writes=[Bf["No1"]])
        fw.op(fw.dve, lambda: nc.vector.tensor_tensor(out=gv(Mo[1]), in0=gv(Y0), in1=mkl(1, 1), op=ALU.mult), reads=[Bf["Y0"], self.b_masks], writes=[Bf["Mo1"]])
        yield
        cur = 1
        for lv in range(1, L):
            nx = 1 - cur
            mp = lv % 2
            last = (lv == L - 1)
            if not last:
                for hh in range(G):
                    fw.op(fw.pe, lambda hh=hh, cur=cur, mp=mp: nc.tensor.matmul(P[pa][0:C, hh * C:(hh + 1) * C], hsl(No[mp], hh), hsl(A[cur], hh), start=True, stop=True),
                          reads=[Bf[nNo[mp]], Bf[nA[cur]]], writes=[bP[pa]], inc=(hh == G - 1))
            for hh in range(G):
                fw.op(fw.pe, lambda hh=hh, cur=cur, mp=mp: nc.tensor.matmul(P[pb_][0:C, hh * C:(hh + 1) * C], hsl(Mo[mp], hh), hsl(B[cur], hh), start=True, stop=True),
                      reads=[Bf[nMo[mp]], Bf[nB[cur]]], writes=[bP[pb_]], inc=(hh == G - 1))
            if lv + 1 < L:
                m2 = (lv + 1) % 2
                if lv + 1 < L - 1:
                    fw.op(fw.pool, lambda m2=m2, lv=lv: nc.gpsimd.tensor_tensor(out=gv(No[m2]), in0=gv(X0), in1=mkl(0, lv + 1), op=ALU.mult),
                          reads=[Bf["X0"], self.b_masks], writes=[Bf[nNo[m2]]])
                fw.op(fw.pool, lambda m2=m2, lv=lv: nc.gpsimd.tensor_tensor(out=gv(Mo[m2]), in0=gv(Y0), in1=mkl(1, lv + 1), op=ALU.mult),
                      reads=[Bf["Y0"], self.b_masks], writes=[Bf[nMo[m2]]])
            yield
            if not last:
                fw.op(fw.act, lambda: nc.scalar.copy(g2(Q[0]), P[pa][0:C, 0:G * C]), reads=[bP[pa]], writes=[Bf["Q0"]])
            fw.op(fw.dve, lambda: nc.vector.tensor_copy(g2(Q[1]), P[pb_][0:C, 0:G * C]), reads=[bP[pb_]], writes=[Bf["Q1"]])
            idm = self.cbf[0:C, 0:C]
            if not last:
                for hh in range(G):
                    fw.op(fw.pe, lambda hh=hh, cur=cur: nc.tensor.matmul(P[pa][0:C, hh * C:(hh + 1) * C], idm, hsl(A[cur], hh), start=True, stop=False),
                          reads=[Bf[nA[cur]], self.b_cbf], writes=[bP[pa]], inc=False)
                    fw.op(fw.pe, lambda hh=hh, cur=cur: nc.tensor.matmul(P[pa][0:C, hh * C:(hh + 1) * C], hsl(B[cur], hh), hsl(Q[0], hh), start=False, stop=True),
                          reads=[Bf[nB[cur]], Bf["Q0"]], writes=[bP[pa]], inc=(hh == G - 1))
            for hh in range(G):
                fw.op(fw.pe, lambda hh=hh, cur=cur: nc.tensor.matmul(P[pb_][0:C, hh * C:(hh + 1) * C], idm, hsl(B[cur], hh), start=True, stop=False),
                      reads=[Bf[nB[cur]], self.b_cbf], writes=[bP[pb_]], inc=False)
                fw.op(fw.pe, lambda hh=hh, cur=cur: nc.tensor.matmul(P[pb_][0:C, hh * C:(hh + 1) * C], hsl(A[cur], hh), hsl(Q[1], hh), start=False, stop=True),
                      reads=[Bf[nA[cur]], Bf["Q1"]], writes=[bP[pb_]], inc=(hh == G - 1))
            yield
            if not last:
                fw.op(fw.act, lambda cur=cur, nx=nx: nc.scalar.copy(g2(A[nx]), P[pa][0:C, 0:G * C]), reads=[bP[pa]], writes=[Bf[nA[nx]]])
                fw.op(fw.dve, lambda cur=cur, nx=nx: nc.vector.tensor_copy(g2(B[nx]), P[pb_][0:C, 0:G * C]), reads=[bP[pb_]], writes=[Bf[nB[nx]]])
            else:
                fw.op(fw.act, lambda cur=cur: nc.scalar.copy(g2(ATp), P[pb_][0:C, 0:G * C]), reads=[bP[pb_]], writes=[bATp])
            cur = nx
            yield

    def gdn_phaseB(self, tl, g, n_, chunk):
        fw, nc = self.fw, self.nc
        co, sg, cfirst, clast = chunk
        C = tl["C"]
        G = 4
        gb = self.gb
        P, bP = self.P, self.b_P
        par = n_ % 4
        V = self._gviews(tl, g, n_)
        hv, h2, hsl, colb, psv = V["hv"], V["h2"], V["hsl"], V["colb"], V["psv"]
        ATp, QKDp = self.ATs[par], self.QKDs[par]
        bATp, bQKDp, bdl = gb[f"AT{par}"], gb[f"QKD{par}"], gb[f"dl{par}"]
        dlast, gtot = self.dlasts[par], self.gtots[par]
        Sg = self.S[:, 4 * g:4 * g + 4, :]
        bS = self.b_S[g]
        hs = sg["seq"]
        if cfirst and tl["kind"] == "s":
            fw.dma(fw.sp, Sg, self.sgdn[hs - 1, 4 * g:4 * g + 4].rearrange("h k v -> k h v"), self.d_S[g], writes=[bS])
        fw.op(fw.dve, lambda: nc.vector.tensor_copy(self.Sbf[:, 0:G * HD].rearrange("p (g v) -> p g v", g=G), Sg), reads=[bS], writes=[gb["Sbf"]])
        fw.op(fw.pool, lambda: nc.gpsimd.tensor_tensor(out=Sg, in0=Sg, in1=bass.AP(gtot, 0, [[G, 128], [1, G], [0, HD]]), op=ALU.mult),
              reads=[bS, bdl], writes=[bS])
        for jj in range(2):
            kT = self.CH[:, 18 + jj, co:co + C]
            qT = self.CH[:, 16 + jj, co:co + C]
            srhs = self.Sbf[:, 2 * jj * HD:(2 * jj + 2) * HD]
            fw.op(fw.pe, lambda jj=jj, kT=kT, srhs=srhs: nc.tensor.matmul(P[3][0:C, jj * 256:(jj + 1) * 256], kT, srhs, start=True, stop=True),
                  reads=[self.b_CH[18 + jj], gb["Sbf"]], writes=[bP[3]], inc=(jj == 1))
            fw.op(fw.pe, lambda jj=jj, qT=qT, srhs=srhs: nc.tensor.matmul(P[4][0:C, jj * 256:(jj + 1) * 256], qT, srhs, start=True, stop=True),
                  reads=[self.b_CH[16 + jj], gb["Sbf"]], writes=[bP[4]], inc=(jj == 1))
        for hh in range(G):
            fw.op(fw.pe, lambda hh=hh: nc.tensor.transpose(self.PB[0:C, 512 + hh * HD:512 + (hh + 1) * HD], self.CH[:, 20 + hh, co:co + C], self.cbf[:, 0:128]),
                  reads=[self.b_CH[20 + hh], self.b_cbf], writes=[self.b_PB2], inc=(hh == G - 1))
        kS = psv(P[3], HD)
        qS = psv(P[4], HD)
        vtok = self.PB[0:C, 512:512 + G * HD].rearrange("p (g v) -> p g v", g=G)
        fw.op(fw.dve, lambda: nc.vector.tensor_tensor(out=hv(self.t1), in0=kS, in1=colb(self.egc, HD), op=ALU.mult), reads=[bP[3], gb["egc"]], writes=[gb["t1"]])
        fw.op(fw.dve, lambda: nc.vector.tensor_tensor(out=hv(self.t1), in0=vtok, in1=hv(self.t1), op=ALU.subtract), reads=[self.b_PB2, gb["t1"]], writes=[gb["t1"]])
        yield
        fw.op(fw.dve, lambda: nc.vector.tensor_tensor(out=hv(self.rr), in0=hv(self.t1), in1=colb(self.beta, HD), op=ALU.mult), reads=[gb["t1"], gb["beta"]], writes=[gb["rr"]])
        for jj in range(2):
            fw.op(fw.pe, lambda jj=jj: nc.tensor.transpose(self.PB[0:C, 512 + jj * HD:512 + (jj + 1) * HD], self.CH[:, 18 + jj, co:co + C], self.cbf[:, 0:128]),
                  reads=[self.b_CH[18 + jj], self.b_cbf], writes=[self.b_PB2], inc=(jj == 1))
        fw.op(fw.act, lambda: nc.scalar.copy(self.ktok[0:C, :], self.PB[0:C, 512:512 + 2 * HD]), reads=[self.b_PB2], writes=[gb["ktok"]])
        for hh in range(G):
            fw.op(fw.pe, lambda hh=hh: nc.tensor.matmul(P[5][0:C, hh * HD:(hh + 1) * HD], hsl(ATp, hh), hsl(self.rr, hh, HD), start=True, stop=True),
                  reads=[bATp, gb["rr"]], writes=[bP[5]], inc=(hh == G - 1))
        yield
        fw.op(fw.dve, lambda: nc.vector.tensor_tensor(out=hv(self.vnd), in0=psv(P[5], HD), in1=bass.AP(dlast, 0, [[G, C], [1, G], [0, HD]]), op=ALU.mult),
              reads=[bP[5], bdl], writes=[gb["vnd"]])
        fw.op(fw.act, lambda: nc.scalar.copy(h2(self.vn), P[5][0:C, 0:G * HD]), reads=[bP[5]], writes=[gb["vn"]])
        for jj in range(2):
            fw.op(fw.pe, lambda jj=jj: nc.tensor.matmul(P[5][:, jj * 256:(jj + 1) * 256], self.ktok[0:C, jj * HD:(jj + 1) * HD],
                                                        self.vnd[0:C, 2 * jj * HD:(2 * jj + 2) * HD], start=True, stop=True),
                  reads=[gb["ktok"], gb["vnd"]], writes=[bP[5]], inc=(jj == 1))
        for hh in range(G):
            fw.op(fw.pe, lambda hh=hh: nc.tensor.matmul(P[3][0:C, hh * HD:(hh + 1) * HD], hsl(QKDp, hh), hsl(self.vn, hh, HD), start=True, stop=True),
                  reads=[bQKDp, gb["vn"]], writes=[bP[3]], inc=(hh == G - 1))
        yield
        fw.op(fw.dve, lambda: nc.vector.tensor_tensor(out=Sg, in0=Sg, in1=P[5][:, 0:G * HD].rearrange("p (g v) -> p g v", g=G), op=ALU.add),
              reads=[bS, bP[5]], writes=[bS])
        if clast and sg["last"]:
            dst = self.o_sp if tl["kind"] == "p" else self.o_ss[hs - 1]
            ob = Buf()
            fw.dma(fw.pool, dst[4 * g:4 * g + 4].rearrange("h k v -> k h v"), Sg, self.d_S[g], reads=[bS], writes=[ob])
            self.outbufs.append(ob)
        fw.op(fw.dve, lambda: nc.vector.tensor_tensor(out=hv(self.t2), in0=qS, in1=colb(self.egc, HD), op=ALU.mult), reads=[bP[4], gb["egc"]], writes=[gb["t2"]])
        yield
        fw.op(fw.dve, lambda: nc.vector.tensor_tensor(out=h2(self.oo), in0=h2(self.t2), in1=P[3][0:C, 0:G * HD], op=ALU.add),
              reads=[bP[3], gb["t2"]], writes=[gb["oo"]])
        fw.op(fw.act, lambda: nc.scalar.activation(h2(self.osq), h2(self.oo), AF.Square), reads=[gb["oo"]], writes=[gb["osq"]])
        fw.op(fw.dve, lambda: nc.vector.tensor_reduce(out=self.oss[0:C, 0, :], in_=hv(self.osq), axis=AX.X, op=ALU.add), reads=[gb["osq"]], writes=[gb["oss"]])
        fw.op(fw.act, lambda: nc.scalar.activation(self.oss[0:C, 1, :], self.oss[0:C, 0, :], AF.Ln, bias=self.sc[0:C, 0:1], scale=1.0 / HD),
              reads=[gb["oss"], self.b_sc], writes=[gb["oss"]])
        fw.op(fw.act, lambda: nc.scalar.activation(self.oss[0:C, 1, :], self.oss[0:C, 1, :], AF.Exp, scale=-0.5), reads=[gb["oss"]], writes=[gb["oss"]])
        yield
        fw.op(fw.dve, lambda: nc.vector.tensor_tensor(out=hv(self.oo), in0=hv(self.oo), in1=bass.AP(self.oss, G, [[2 * G, C], [1, G], [0, HD]]), op=ALU.mult),
              reads=[gb["oo"], gb["oss"]], writes=[gb["oo"]])
        for hh in range(G):
            fw.op(fw.pe, lambda hh=hh: nc.tensor.transpose(P[4][:, hh * C:(hh + 1) * C], hsl(self.oo, hh, HD), self.cst[0:C, 0:C]),
                  reads=[gb["oo"], self.b_cst], writes=[bP[4]], inc=(hh == G - 1))
        yield
        for hh in range(G):
            h = 4 * g + hh
            fw.op(fw.dve, lambda hh=hh, h=h: nc.vector.scalar_tensor_tensor(
                out=self.CH[:, h, co:co + C], in0=P[4][:, hh * C:(hh + 1) * C], scalar=self.V("g_norm", 0), in1=self.CH[:, 24 + hh, co:co + C],
                op0=ALU.mult, op1=ALU.mult), reads=[bP[4], self.b_vecs, self.b_CH[24 + hh]], writes=[self.b_CH[h]])

    def load_ghalo(self, sg):
        fw, nc = self.fw, self.nc
        hs = sg["seq"]
        HL = SCW - 1
        for q in range(4):
            fw.dma(fw.sp, self.cin[0:HL, :], self.cgc[hs - 1, :, q * 1024:(q + 1) * 1024], self.d_cin, writes=[self.b_cin])
            P = self.P[6]
            for c in range(8):
                fw.op(fw.pe, lambda c=c: nc.tensor.transpose(P[:, c * 4:c * 4 + HL], self.cin[0:HL, c * 128:(c + 1) * 128], self.ident[0:HL, 0:HL]),
                      reads=[self.b_cin, self.b_cst], writes=[self.b_P[6]], inc=(c == 7))
            fw.op(fw.dve, lambda q=q: nc.vector.tensor_copy(self.ghalo[:, hs, q * 8:(q + 1) * 8, :], P[:, 0:32].rearrange("p (c h) -> p c h", c=8)[:, :, 0:HL]),
                  reads=[self.b_P[6]], writes=self.b_ghalo[hs][q * 8:(q + 1) * 8])


_CACHE = {}


def _get_prog(n_ptiles=16, do_sample=True):
    key = (n_ptiles, do_sample)
    if key not in _CACHE:
        _CACHE[key] = Prog(n_ptiles, do_sample)
    return _CACHE[key]


def make_in_maps(inp):
    wall = build_wall(inp)
    vecs = build_vecs(inp)
    consts = build_consts()
    masks = build_masks(128)
    hrow = np.concatenate([np.asarray(inp["gdn_A_log"][0], np.float32), np.asarray(inp["gdn_dt_bias"][0], np.float32)])[None, :]
    maps = []
    for i in range(8):
        s = slice(NSMP * i, NSMP * i + NSMP)
        maps.append({
            "xp": np.ascontiguousarray(inp["x_prompt"][i]),
            "xs": np.ascontiguousarray(np.asarray(inp["x_sample"][s]).reshape(NSMP * DSEQ, D)),
            "c5": np.ascontiguousarray(np.concatenate([np.asarray(inp["c_prompt"][i:i + 1]), np.asarray(inp["c_sample"][s])], 0)),
            "cconv": np.ascontiguousarray(inp["cache_conv"][0, s]),
            "sgdn": np.ascontiguousarray(inp["state_gdn"][0, s]),
            "cgc": np.ascontiguousarray(inp["cache_gdn_conv"][0, s]),
            "wall": wall, "vecs": vecs, "hrow": np.ascontiguousarray(hrow), "consts": consts, "masks": masks,
        })
    return maps


def kernel(**inp):
    inp = {k: np.asarray(v) for k, v in inp.items()}
    prog = _get_prog()
    maps = make_in_maps(inp)
    res = run_bass_kernel_spmd(prog.nc, maps, core_ids=list(range(8)))
    R = res.results
    f = np.float32
    y_p = np.stack([R[i]["yp"] for i in range(8)]).astype(f)
    y_s = np.concatenate([R[i]["ys"].reshape(NSMP, DSEQ, D) for i in range(8)]).astype(f)
    ccp = np.stack([R[i]["ccp"] for i in range(8)])[None].astype(f)
    ccs = np.concatenate([R[i]["ccs"] for i in range(8)])[None].astype(f)
    s_p = np.stack([R[i]["sp"] for i in range(8)])[None].astype(f)
    s_s = np.concatenate([R[i]["ss"] for i in range(8)])[None].astype(f)
    gcp = np.stack([R[i]["gcp"] for i in range(8)])[None].astype(f)
    gcs = np.concatenate([R[i]["gcs"] for i in range(8)])[None].astype(f)
    return (y_p, y_s, ccp, ccs, s_p, s_s, gcp, gcs)
```

```python
from contextlib import ExitStack
import numpy as np
import concourse.bass as bass
import concourse.mybir as mybir
from concourse.bass_utils import run_bass_kernel_spmd

F32 = mybir.dt.float32
BF16 = mybir.dt.bfloat16
AF = mybir.ActivationFunctionType
ALU = mybir.AluOpType
AX = mybir.AxisListType
EPOCH = 24000

D = 1024
FF = 2816
NH = 16
HD = 128
CW = 31
SCW = 4
EPS = 1e-6
SEQ = 8192
DSEQ = 64
NSMP = 4
TP = 512
NSLOT = 3
NA_INFLIGHT = 2
NDG = 8
SEQ_EMIT = False
SLOT_E = 4096
CONV_CH = 128 * 8192


class Buf:
    __slots__ = ("name", "w", "r", "excl")

    def __init__(self, name="", excl=False):
        self.name = name
        self.w = None
        self.r = {}
        self.excl = excl


def bufs(n, name=""):
    return [Buf(f"{name}{i}") for i in range(n)]


class DSem:
    def __init__(self, h):
        self.h = h
        self.count = 0


class Eng:
    def __init__(self, fw, name, e):
        self.fw = fw
        self.name = name
        self.e = e
        self.count = 0
        self.sems = []
        self.known = {}
        self.known_dma = {}

    def sem_for(self, n):
        idx = (n - 1) // EPOCH
        while len(self.sems) <= idx:
            self.sems.append(self.fw.new_sem(f"{self.name}{len(self.sems)}"))
        return self.sems[idx], n - idx * EPOCH


class FW:
    def __init__(self):
        self.nc = bass.Bass("TRN2", target_bir_lowering=False)
        self.es = ExitStack()
        nc = self.nc
        self.pe = Eng(self, "pe", nc.tensor)
        self.act = Eng(self, "act", nc.scalar)
        self.dve = Eng(self, "dve", nc.vector)
        self.pool = Eng(self, "pool", nc.gpsimd)
        self.sp = Eng(self, "sp", nc.sync)
        self.nsem = 0
        self.ntile = 0

    def new_sem(self, name):
        self.nsem += 1
        return self.es.enter_context(self.nc.semaphore(f"s{self.nsem}_{name}"))

    def dsem(self, name="d"):
        return DSem(self.new_sem(name))

    def sb(self, name, shape, dt):
        self.ntile += 1
        return self.es.enter_context(self.nc.sbuf_tensor(f"{name}_{self.ntile}", list(shape), dt))

    def ps(self, name, shape, dt=F32):
        self.ntile += 1
        return self.es.enter_context(self.nc.psum_tensor(f"{name}_{self.ntile}", list(shape), dt))

    def _wait(self, E, evs):
        for ev in evs:
            if ev is None:
                continue
            if ev[0] == "eng":
                _, oname, seq = ev
                if oname == E.name and E.name in ("pe", "sp"):
                    continue
                if E.known.get(oname, 0) >= seq:
                    continue
                O = getattr(self, oname)
                sem, val = O.sem_for(seq)
                E.e.wait_ge(sem, val)
                E.known[oname] = seq
            else:
                _, ds, val = ev
                if E.known_dma.get(id(ds), 0) >= val:
                    continue
                E.e.wait_ge(ds.h, val)
                E.known_dma[id(ds)] = val

    @staticmethod
    def _deps(reads, writes):
        evs = []
        for b in reads:
            if b.w is not None:
                evs.append(b.w)
            if b.excl:
                evs.extend(b.r.values())
        for b in writes:
            if b.w is not None:
                evs.append(b.w)
            evs.extend(b.r.values())
        return evs

    def op(self, E, fn, reads=(), writes=(), inc=True):
        self._wait(E, self._deps(reads, writes))
        ins = fn()
        if inc:
            E.count += 1
            sem, _ = E.sem_for(E.count)
            ins.then_inc(sem, 1)
            ev = ("eng", E.name, E.count)
        else:
            ev = ("eng", E.name, E.count + 1)
        for b in reads:
            b.r[E.name] = ev
        for b in writes:
            b.w = ev
            b.r = {}
        return ins

    def dma(self, Q, out_ap, in_ap, ds, reads=(), writes=(), **kw):
        self._wait(Q, self._deps(reads, writes))
        ins = Q.e.dma_start(out=out_ap, in_=in_ap, **kw)
        ds.count += 16
        ins.then_inc(ds.h, 16)
        ev = ("dma", ds, ds.count)
        for b in reads:
            b.r[("dma", id(ds))] = ev
        for b in writes:
            b.w = ev
            b.r = {}
        return ins

    def finish(self, blist):
        evs = []
        for b in blist:
            if b.w is not None:
                evs.append(b.w)
            evs.extend(b.r.values())
        for E in (self.pe, self.act, self.dve, self.pool):
            if E.count:
                evs.append(("eng", E.name, E.count))
        self._wait(self.sp, evs)


def tile_block_list():
    bl = []
    for b in range(4):
        bl.append(("pw1", b, 8, 512))
    for b in range(2):
        bl.append(("pw2", b, 8, 512))
    for b in range(11):
        bl.append(("gu0", b, 8, 512))
    for b in range(8):
        bl.append(("dn0", b, 22, 128))
    bl.append(("gab", 0, 8, 32))
    for g in range(4):
        for i in range(3):
            bl.append(("gin", g * 3 + i, 8, 512))
    for b in range(8):
        bl.append(("gout", b, 16, 128))
    for b in range(11):
        bl.append(("gu1", b, 8, 512))
    for b in range(8):
        bl.append(("dn1", b, 22, 128))
    return bl


def ada_block_list():
    return [("ada", l * 12 + b, 8, 512) for l in range(2) for b in range(12)]


def block_offsets():
    offs = {}
    o = 0
    for blk in ada_block_list() + tile_block_list():
        offs[(blk[0], blk[1])] = o
        o += 128 * blk[2] * blk[3]
    total = ((o + CONV_CH - 1) // CONV_CH) * CONV_CH
    return offs, total


def _arr(Wc):
    K, n = Wc.shape
    KC = K // 128
    return np.ascontiguousarray(Wc.reshape(KC, 128, n).transpose(1, 0, 2)).reshape(-1)


def build_wall(inp):
    offs, total = block_offsets()
    wall = np.zeros(total, np.float32)

    def put(key, Wc):
        a = _arr(np.asarray(Wc, np.float32))
        wall[offs[key]:offs[key] + a.size] = a

    for l in range(2):
        for b in range(12):
            put(("ada", l * 12 + b), inp["w_ada"][l][:, 512 * b:512 * b + 512])
    w1 = inp["w_pw1"][0]
    for b in range(4):
        put(("pw1", b), np.concatenate([w1[:, 256 * b:256 * b + 256], w1[:, 1024 + 256 * b:1024 + 256 * b + 256]], 1))
    for b in range(2):
        put(("pw2", b), inp["w_pw2"][0][:, 512 * b:512 * b + 512])
    for l in range(2):
        g, u, d = inp["w_ffn_gate"][l], inp["w_ffn_up"][l], inp["w_ffn_down"][l]
        for b in range(11):
            put((f"gu{l}", b), np.concatenate([g[:, 256 * b:256 * b + 256], u[:, 256 * b:256 * b + 256]], 1))
        for b in range(8):
            put((f"dn{l}", b), d[:, 128 * b:128 * b + 128])
    wi = inp["w_gdn_in"][0]
    put(("gab", 0), wi[:, 6144:6176])
    for g in range(4):
        put(("gin", g * 3 + 0), np.concatenate([wi[:, 256 * g:256 * g + 256], wi[:, 1024 + 256 * g:1024 + 256 * g + 256]], 1))
        put(("gin", g * 3 + 1), wi[:, 2048 + 512 * g:2048 + 512 * g + 512])
        put(("gin", g * 3 + 2), wi[:, 4096 + 512 * g:4096 + 512 * g + 512])
    wo = inp["w_gdn_out"][0]
    for b in range(8):
        put(("gout", b), wo[:, 128 * b:128 * b + 128])
    return wall


VC = {}
_o = 0
for _n, _w in [("g_pre_mix", 16), ("g_post_mix", 16), ("g_pre_ffn", 16), ("g_post_ffn", 16), ("b_ada", 96),
               ("b_pw1", 16), ("b_dw", 8), ("ln_g", 8), ("ln_b", 8), ("b_pw2", 8), ("w_dw", 248), ("w_gc", 128),
               ("g_norm", 1)]:
    VC[_n] = _o
    _o += _w
NVEC = _o


def fm(v):
    v = np.asarray(v, np.float32)
    return np.ascontiguousarray(v.reshape(-1, 128).T)


def build_vecs(inp):
    V = np.zeros((128, NVEC), np.float32)
    for n in ["g_pre_mix", "g_post_mix", "g_pre_ffn", "g_post_ffn"]:
        for l in range(2):
            V[:, VC[n] + 8 * l:VC[n] + 8 * l + 8] = fm(inp[n][l])
    for l in range(2):
        V[:, VC["b_ada"] + 48 * l:VC["b_ada"] + 48 * l + 48] = fm(inp["b_ada"][l])
    V[:, VC["b_pw1"]:VC["b_pw1"] + 16] = fm(inp["b_pw1"][0])
    V[:, VC["b_dw"]:VC["b_dw"] + 8] = fm(inp["b_dw"][0])
    V[:, VC["ln_g"]:VC["ln_g"] + 8] = fm(inp["ln_conv_g"][0])
    V[:, VC["ln_b"]:VC["ln_b"] + 8] = fm(inp["ln_conv_b"][0])
    V[:, VC["b_pw2"]:VC["b_pw2"] + 8] = fm(inp["b_pw2"][0])
    wd = np.asarray(inp["w_dw"][0], np.float32)
    V[:, VC["w_dw"]:VC["w_dw"] + 248] = wd.reshape(CW, 8, 128).transpose(2, 1, 0).reshape(128, 248)
    wg = np.asarray(inp["w_gdn_conv"][0], np.float32)
    V[:, VC["w_gc"]:VC["w_gc"] + 128] = wg.reshape(SCW, 32, 128).transpose(2, 1, 0).reshape(128, 128)
    V[:, VC["g_norm"]] = np.asarray(inp["g_gdn_out_norm"][0], np.float32)
    return V


def build_masks(C):
    L = int(np.log2(C))
    M = np.zeros((128, 2, L, C), np.float32)
    i = np.arange(C)[:, None]
    j = np.arange(C)[None, :]
    for l in range(L):
        b = 1 << l
        same = (i // (2 * b)) == (j // (2 * b))
        M[:C, 0, l, :] = same & ((i % (2 * b)) < b) & ((j % (2 * b)) >= b)
        M[:C, 1, l, :] = same & ((i % (2 * b)) >= b) & ((j % (2 * b)) < b)
    return M.reshape(128, 2 * L * C)


def build_consts():
    C = np.zeros((128, 5, 128), np.float32)
    s = np.arange(128)[:, None]
    c = np.arange(128)[None, :]
    C[:, 0, :] = (s == c)
    C[:, 1, :] = 1.0
    C[:, 2, :] = (s <= c)
    C[:, 3, :] = np.where(c >= s, 0.0, -1e30)
    C[:, 4, :] = (c > s)
    return C.reshape(128, 640)


class StopBuild(Exception):
    pass


class Prog:
    def __init__(self, n_ptiles=16, do_sample=True, C=128, stop=None):
        self.stop = stop
        self.n_ptiles = n_ptiles
        self.do_sample = do_sample
        self.C = C
        self.fw = FW()
        self.nc = self.fw.nc
        self.build()

    def V(self, name, col, n=1):
        c0 = VC[name] + col
        return self.vecs[:, c0:c0 + n]

    def build(self):
        fw, nc = self.fw, self.nc
        offs, total = block_offsets()
        self.offs = offs
        di = lambda n, s: nc.dram_tensor(n, list(s), F32, kind="ExternalInput")
        do = lambda n, s: nc.dram_tensor(n, list(s), F32, kind="ExternalOutput")
        self.xp = di("xp", [SEQ, D])
        self.xs = di("xs", [NSMP * DSEQ, D])
        self.c5 = di("c5", [1 + NSMP, D])
        self.cconv = di("cconv", [NSMP, CW - 1, D])
        self.sgdn = di("sgdn", [NSMP, NH, HD, HD])
        self.cgc = di("cgc", [NSMP, SCW - 1, 4096])
        self.wall = di("wall", [total])
        self.vecs_d = di("vecs", [128, NVEC])
        self.hrow_d = di("hrow", [1, 32])
        self.consts_d = di("consts", [128, 640])
        self.masks_d = di("masks", [128, 2 * 7 * 128])
        self.yp = do("yp", [SEQ, D])
        self.ys = do("ys", [NSMP * DSEQ, D])
        self.o_ccp = do("ccp", [CW - 1, D])
        self.o_ccs = do("ccs", [NSMP, CW - 1, D])
        self.o_sp = do("sp", [NH, HD, HD])
        self.o_ss = do("ss", [NSMP, NH, HD, HD])
        self.o_gcp = do("gcp", [SCW - 1, 4096])
        self.o_gcs = do("gcs", [NSMP, SCW - 1, 4096])
        self.wsc = nc.dram_tensor("wsc", [total], BF16, kind="Internal")
        self.outbufs = []

        sb = fw.sb
        self.cst = sb("cst", [128, 640], F32)
        self.cbf = sb("cbf", [128, 384], BF16)
        self.vecs = sb("vecs", [128, NVEC], F32)
        self.hrow = sb("hrow", [128, 32], F32)
        self.sc = sb("sc", [128, 8], F32)
        self.coef = sb("coef", [128, 2 * 6 * 8 * 5], F32)
        self.xT = sb("xT", [128, 8, TP], F32)
        self.hT = sb("hT", [128, 8, TP], BF16)
        self.tmp = sb("tmp", [128, 8, TP], F32)
        self.CH = sb("CH", [128, 28, TP], BF16)
        self.sq = self.CH[:, 20:28, :]
        self.u = sb("u", [128, 8, CW - 1 + TP], BF16)
        self.S = sb("S", [128, NH, HD], F32)
        self.wslot = [sb(f"ws{i}", [128, SLOT_E], BF16) for i in range(NSLOT)]
        self.xst = [sb(f"xst{i}", [128, D], F32) for i in range(2)]
        self.ucf = self.xst[1][:, 0:NSMP * 8 * (CW - 1)].rearrange("p (s c h) -> p s c h", s=NSMP, c=8)
        self.ost = [sb(f"ost{i}", [128, D], F32) for i in range(1)]
        self.rs = sb("rs", [128, 2, TP], F32)
        self.dg = self.rs[:].bitcast(BF16).rearrange("p a b -> p (a b)")[:, 0:NDG * 128].rearrange("p (s c) -> p s c", s=NDG)
        self.modt = self.rs[:].rearrange("p a b -> p (a b)")[:, 0:480].rearrange("p (a s) -> p a s", s=5)
        self.sg = [sb(f"sg{i}", [128, TP], F32) for i in range(2)]
        self.ghalo = sb("ghalo", [128, 1 + NSMP, 32, SCW - 1], F32)
        self.b_dg = bufs(NDG)
        self.cstg = [sb(f"cstg{i}", [128, 544], F32) for i in range(2)]
        self.cvo = self.sg
        self.qkf = [sb(f"qkf{i}", [128, TP], F32) for i in range(2)]
        self.sq1 = [sb(f"sq1{i}", [128, TP], BF16) for i in range(2)]
        self.scT = sb("scT", [128, 8, 8], BF16)
        self.cin = self.ost[0][0:32, :]
        self.c5s = self.cin

        B = Buf
        self.b_cst, self.b_cbf, self.b_vecs, self.b_hrow, self.b_sc, self.b_coef = B(), B(), B(), B(), B(), B()
        self.b_xT = bufs(8)
        self.b_hT = bufs(8)
        self.b_tmp = bufs(8)
        self.b_CH = bufs(28)
        self.b_sq = self.b_CH[20:28]
        self.b_u = bufs(8)
        self.b_S = bufs(4)
        self.b_ws = bufs(NSLOT)
        self.d_ws = [fw.dsem("ws") for _ in range(NSLOT)]
        self.b_xst = bufs(2)
        self.b_ucf = [self.b_xst[1]] * NSMP
        self.d_xst = [fw.dsem("xst") for _ in range(2)]
        self.b_ost = bufs(1)
        self.b_cin = self.b_ost[0]
        self.d_ost = [fw.dsem("ost") for _ in range(1)]
        self.b_rs = bufs(2)
        self.b_sg = bufs(2)
        self.b_ghalo = [bufs(32) for _ in range(1 + NSMP)]
        self.b_cstg = bufs(2)
        self.b_cvo = self.b_sg
        self.b_qkf = bufs(2)
        self.b_sq1 = bufs(2)
        self.d_cin = fw.dsem("cin")
        self.d_misc = [fw.dsem("misc") for _ in range(6)]
        self.d_yst = [fw.dsem("yst") for _ in range(4)]
        self.d_S = [fw.dsem("S") for _ in range(4)]

        self.P = [fw.ps(f"P{i}", [128, 512], F32) for i in range(7)]
        self.PB = fw.ps("PB", [128, 1024], BF16)
        self.b_P = [Buf(f"P{i}", excl=True) for i in range(7)]
        self.b_PB = Buf("PB", excl=True)
        self.b_PB2 = self.b_PB

        self.gdn_alloc()
        self.dg2i = 0
        self.b_dg2 = bufs(8)

        self.load_consts()
        self.convert_weights(total)
        self.wq = ada_block_list()
        ntiles = self.n_ptiles + (1 if self.do_sample else 0)
        tb = tile_block_list()
        if self.stop == "loadx":
            tb = []
        elif self.stop == "l0":
            tb = [b for b in tb if b[0] in ("pw1", "pw2")]
        elif self.stop == "ffn0":
            tb = [b for b in tb if b[0] in ("pw1", "pw2", "gu0", "dn0")]
        elif self.stop == "gdn":
            tb = [b for b in tb if b[0] not in ("gu1", "dn1")]
        for _ in range(ntiles):
            self.wq += tb
        self.wi = 0
        self.wl = 0
        if self.stop == "conv":
            fw.finish(self.b_conv + [self.b_cst, self.b_vecs, self.b_hrow, self.b_cin, self.b_cbf, self.b_sc])
            fw.es.close()
            return
        self.adaln()
        if self.stop == "adaln":
            fw.finish([self.b_coef])
            fw.es.close()
            return
        self.tiles = []
        for t in range(self.n_ptiles):
            self.tiles.append(dict(kind="p", t=t, T=TP, C=self.C, segs=[dict(seq=0, n=TP, off=0, first=(t == 0), last=(t == SEQ // TP - 1))]))
        if self.do_sample:
            self.tiles.append(dict(kind="s", t=0, T=NSMP * DSEQ, C=64,
                                   segs=[dict(seq=1 + s, n=DSEQ, off=s * DSEQ, first=True, last=True) for s in range(NSMP)]))
        self.prefetch_x(self.tiles[0])
        try:
            self.run_tiles()
        except StopBuild:
            pass
        fw.finish(self.outbufs)
        fw.es.close()

    def run_tiles(self):
        for i, tl in enumerate(self.tiles):
            self.load_x(tl)
            if self.stop != "loadx":
                self.layer0(tl)
            if self.stop not in ("loadx", "l0"):
                self.ffn(tl, 0)
            if self.stop not in ("loadx", "l0", "ffn0"):
                self.gdn(tl)
            if self.stop not in ("loadx", "l0", "ffn0", "gdn"):
                self.ffn(tl, 1)
            if i + 1 < len(self.tiles):
                self.prefetch_x(self.tiles[i + 1])
            self.store_y(tl)

    def load_consts(self):
        fw, nc = self.fw, self.nc
        fw.dma(fw.sp, self.cst[:], self.consts_d[:], self.d_misc[0], writes=[self.b_cst])
        fw.dma(fw.sp, self.vecs[:], self.vecs_d[:], self.d_misc[1], writes=[self.b_vecs])
        fw.dma(fw.sp, self.hrow[:], self.hrow_d[0:1, :].broadcast_to([128, 32]), self.d_misc[2], writes=[self.b_hrow])
        fw.dma(fw.sp, self.c5s[0:1 + NSMP, :], self.c5[:], self.d_misc[3], writes=[self.b_cin])
        fw.op(fw.dve, lambda: nc.vector.tensor_copy(self.cbf[:], self.cst[:, 0:384]), reads=[self.b_cst], writes=[self.b_cbf])
        nm = 2 * 7 * 128
        tv = self.tmp[:].rearrange("p a b -> p (a b)")[:, 0:nm]
        fw.dma(fw.sp, tv, self.masks_d[:], self.d_misc[4], writes=self.b_tmp)
        fw.op(fw.dve, lambda: nc.vector.tensor_copy(self.masks[:], tv), reads=self.b_tmp, writes=[self.b_masks])
        fw.op(fw.dve, lambda: nc.vector.memset(self.sc[:, 0:1], EPS), writes=[self.b_sc])
        fw.op(fw.dve, lambda: nc.vector.memset(self.sc[:, 1:2], 1.0), writes=[self.b_sc])
        fw.op(fw.dve, lambda: nc.vector.memset(self.sc[:, 2:3], 0.0), writes=[self.b_sc])
        fw.op(fw.act, lambda: nc.scalar.activation(self.hrow[:, 0:16], self.hrow[:, 0:16], AF.Exp), reads=[self.b_hrow], writes=[self.b_hrow])
        fw.op(fw.dve, lambda: nc.vector.tensor_scalar(self.hrow[:, 0:16], self.hrow[:, 0:16], -1.0, None, ALU.mult),
              reads=[self.b_hrow], writes=[self.b_hrow])
        for g in range(4):
            fw.op(fw.dve, lambda g=g: nc.vector.memset(self.S[:, 4 * g:4 * g + 4, :], 0.0), writes=[self.b_S[g]])
        fw.op(fw.dve, lambda: nc.vector.memset(self.ghalo[:, 0, :, :], 0.0), writes=self.b_ghalo[0])
        fw.op(fw.dve, lambda: nc.vector.memset(self.u[:, :, 0:CW - 1], 0.0), writes=self.b_u)

    @property
    def ident(self):
        return self.cst[:, 0:128]

    @property
    def ident_bf(self):
        return self.cbf[:, 0:128]

    @property
    def ones_bf(self):
        return self.cbf[:, 128:256]

    def convert_weights(self, total):
        fw = self.fw
        nch = total // CONV_CH
        self.b_conv = bufs(nch)
        self.d_conv = [fw.dsem("cv") for _ in range(nch)]
        for i in range(nch):
            src = bass.AP(self.wall, i * CONV_CH, [[8192, 128], [2048, 4], [1, 2048]])
            dst = bass.AP(self.wsc, i * CONV_CH, [[8192, 128], [2048, 4], [1, 2048]])
            if i >= 6:
                fw._wait(fw.pool, [self.b_conv[i - 6].w])
            fw.dma(fw.pool, dst, src, self.d_conv[i], writes=[self.b_conv[i]])

    def _issue_load(self, j):
        fw = self.fw
        name, idx, KC, ncol = self.wq[j]
        off = self.offs[(name, idx)]
        n = KC * ncol
        s = j % NSLOT
        c0 = off // CONV_CH
        c1 = (off + 128 * n - 1) // CONV_CH
        src = bass.AP(self.wsc, off, [[n, 128], [1, n]])
        fw.dma(fw.sp, self.wslot[s][:, 0:n], src, self.d_ws[s],
               reads=[self.b_conv[c] for c in range(c0, c1 + 1)], writes=[self.b_ws[s]])

    def wnext(self, expect):
        while self.wl < len(self.wq) and self.wl < self.wi + NSLOT:
            self._issue_load(self.wl)
            self.wl += 1
        name, idx, KC, ncol = self.wq[self.wi]
        assert name == expect, (name, expect)
        s = self.wi % NSLOT
        self.wi += 1
        return self.wslot[s][:, 0:KC * ncol].rearrange("p (k n) -> p k n", k=KC), self.b_ws[s]

    def adaln(self):
        fw, nc = self.fw, self.nc
        NS = 1 + NSMP
        fw.op(fw.act, lambda: nc.scalar.activation(self.c5s[0:NS, :], self.c5s[0:NS, :], AF.Silu), reads=[self.b_cin], writes=[self.b_cin])
        P = self.P[0]
        for c in range(8):
            fw.op(fw.pe, lambda c=c: nc.tensor.transpose(P[:, c * 8:c * 8 + NS], self.c5s[0:NS, c * 128:(c + 1) * 128], self.ident[0:NS, 0:NS]),
                  reads=[self.b_cin, self.b_cst], writes=[self.b_P[0]], inc=(c == 7))
        b_scT = Buf()
        fw.op(fw.dve, lambda: nc.vector.tensor_copy(self.scT[:, :, 0:NS], P[:, 0:64].rearrange("p (c s) -> p c s", c=8)[:, :, 0:NS]),
              reads=[self.b_P[0]], writes=[b_scT])
        Pm = self.P[1]
        for l in range(2):
            for b in range(12):
                w, bw = self.wnext("ada")
                for jj in range(4):
                    oc = b * 4 + jj
                    col = (l * 48 + oc) * NS
                    for k in range(8):
                        fw.op(fw.pe, lambda jj=jj, k=k, col=col, w=w: nc.tensor.matmul(
                            Pm[:, col:col + NS], w[:, k, jj * 128:(jj + 1) * 128], self.scT[:, k, 0:NS], start=(k == 0), stop=(k == 7)),
                            reads=[bw, b_scT], writes=[self.b_P[1]], inc=(k == 7 and jj == 3))
        b_mod = Buf()
        bada = self.V("b_ada", 0, 96)
        fw.op(fw.dve, lambda: nc.vector.tensor_tensor(out=self.modt[:, :, 0:NS], in0=Pm[:, 0:96 * NS].rearrange("p (a s) -> p a s", s=NS),
                                                      in1=bass.AP(self.vecs, VC["b_ada"], [[NVEC, 128], [1, 96], [0, NS]]), op=ALU.add),
              reads=[self.b_P[1], self.b_vecs], writes=[b_mod])
        for l in range(2):
            for kind, (m, gname) in enumerate([(1, "g_pre_mix"), (0, None), (2, "g_post_mix"), (4, "g_pre_ffn"), (3, None), (5, "g_post_ffn")]):
                dst = self.coef[:, (l * 6 + kind) * 40:(l * 6 + kind + 1) * 40].rearrange("p (c s) -> p c s", c=8)
                src = self.modt[:, l * 48 + m * 8:l * 48 + m * 8 + 8, 0:NS]
                if gname is None:
                    fw.op(fw.dve, lambda dst=dst, src=src: nc.vector.tensor_copy(dst, src), reads=[b_mod], writes=[self.b_coef])
                else:
                    gv = bass.AP(self.vecs, VC[gname] + 8 * l, [[NVEC, 128], [1, 8], [0, NS]])
                    if kind in (0, 3):
                        fw.op(fw.dve, lambda dst=dst, src=src, gv=gv: nc.vector.scalar_tensor_tensor(
                            out=dst, in0=src, scalar=1.0, in1=gv, op0=ALU.add, op1=ALU.mult), reads=[b_mod, self.b_vecs], writes=[self.b_coef])
                    else:
                        fw.op(fw.dve, lambda dst=dst, src=src, gv=gv: nc.vector.tensor_tensor(out=dst, in0=src, in1=gv, op=ALU.mult),
                              reads=[b_mod, self.b_vecs], writes=[self.b_coef])

    def cf(self, l, kind, c, seq):
        o = (l * 6 + kind) * 40 + c * 5 + seq
        return self.coef[:, o:o + 1]

    def _rows(self, tl, sub):
        if tl["kind"] == "p":
            r0 = tl["t"] * TP + sub * 128
            return self.xp[r0:r0 + 128, :], self.yp[r0:r0 + 128, :]
        r0 = sub * 128
        return self.xs[r0:r0 + 128, :], self.ys[r0:r0 + 128, :]

    def _xs(self, sub):
        v = self.CH[:].bitcast(F32)[:, 4 * sub:4 * sub + 4, :].rearrange("p a b -> p (a b)")
        ds = [self.d_xst[0], self.d_xst[1], self.d_ost[0], self.d_misc[5]][sub]
        return v, self.b_CH[4 * sub:4 * sub + 4], ds

    def prefetch_x(self, tl, half=0):
        fw = self.fw
        for sub in range(tl["T"] // 128):
            src, _ = self._rows(tl, sub)
            v, bb, ds = self._xs(sub)
            fw.dma(fw.sp, v, src, ds, writes=bb)

    def load_x(self, tl):
        fw, nc = self.fw, self.nc
        ns = tl["T"] // 128
        for half in range((ns + 1) // 2):
            n2 = min(2, ns - 2 * half)
            for c in range(8):
                pb = c % 4
                P = self.P[pb]
                for s2 in range(n2):
                    v, bb, _ = self._xs(2 * half + s2)
                    fw.op(fw.pe, lambda c=c, s2=s2, P=P, v=v: nc.tensor.transpose(P[:, s2 * 128:(s2 + 1) * 128], v[:, c * 128:(c + 1) * 128], self.ident),
                          reads=list(bb) + [self.b_cst], writes=[self.b_P[pb]], inc=(s2 == n2 - 1))
                dst = self.xT[:, c, half * 256:half * 256 + n2 * 128]
                if c % 2 == 0:
                    fw.op(fw.dve, lambda dst=dst, P=P, n2=n2: nc.vector.tensor_copy(dst, P[:, 0:n2 * 128]), reads=[self.b_P[pb]], writes=[self.b_xT[c]])
                else:
                    fw.op(fw.act, lambda dst=dst, P=P, n2=n2: nc.scalar.copy(dst, P[:, 0:n2 * 128]), reads=[self.b_P[pb]], writes=[self.b_xT[c]])

    def store_y(self, tl):
        fw, nc = self.fw, self.nc
        ns = tl["T"] // 128
        tst = self.tmp[:].rearrange("p a b -> p (a b)")
        k = 0
        for sub in range(ns):
            for half in range(2):
                pb = k % 4
                k += 1
                P = self.P[pb]
                cidx = 2 * sub + half
                for cc in range(4):
                    c = half * 4 + cc
                    fw.op(fw.pe, lambda c=c, cc=cc, sub=sub, P=P: nc.tensor.transpose(P[:, cc * 128:(cc + 1) * 128], self.xT[:, c, sub * 128:(sub + 1) * 128], self.ident),
                          reads=[self.b_xT[c], self.b_cst], writes=[self.b_P[pb]], inc=(cc == 3))
                dstv = tst[:, cidx * 512:(cidx + 1) * 512]
                if half == 0:
                    fw.op(fw.dve, lambda P=P, dstv=dstv: nc.vector.tensor_copy(dstv, P[:, :]), reads=[self.b_P[pb]], writes=[self.b_tmp[cidx]])
                else:
                    fw.op(fw.act, lambda P=P, dstv=dstv: nc.scalar.copy(dstv, P[:, :]), reads=[self.b_P[pb]], writes=[self.b_tmp[cidx]])
            _, dst = self._rows(tl, sub)
            ob_out = Buf()
            fw.dma(fw.pool, dst, tst[:, sub * 1024:(sub + 1) * 1024], self.d_yst[sub], reads=[self.b_tmp[2 * sub], self.b_tmp[2 * sub + 1]], writes=[ob_out])
            self.outbufs.append(ob_out)

    def stats_sumsq(self, src_aps, src_bufs, T, pbank, nchunks=8):
        fw, nc = self.fw, self.nc
        P = self.P[pbank]
        for c in range(nchunks):
            if c % 2 == 0:
                fw.op(fw.act, lambda c=c: nc.scalar.activation(self.sq[:, c, 0:T], src_aps[:, c, :], AF.Square), reads=[src_bufs[c]], writes=[self.b_sq[c]])
            else:
                fw.op(fw.dve, lambda c=c: nc.vector.tensor_tensor(out=self.sq[:, c, 0:T], in0=src_aps[:, c, :], in1=src_aps[:, c, :], op=ALU.mult),
                      reads=[src_bufs[c]], writes=[self.b_sq[c]])
            fw.op(fw.pe, lambda c=c: nc.tensor.matmul(P[:, 0:T], self.ones_bf, self.sq[:, c, 0:T], start=(c == 0), stop=(c == nchunks - 1)),
                  reads=[self.b_sq[c], self.b_cbf], writes=[self.b_P[pbank]], inc=(c == nchunks - 1))

    def rstd_from(self, psrc_ap, psrc_bufs, T, scale, ri=0):
        fw, nc = self.fw, self.nc
        r = self.rs[:, ri, 0:T]
        fw.op(fw.act, lambda: nc.scalar.activation(r, psrc_ap, AF.Ln, bias=self.sc[:, 0:1], scale=scale), reads=list(psrc_bufs) + [self.b_sc], writes=[self.b_rs[ri]])
        fw.op(fw.act, lambda: nc.scalar.activation(r, r, AF.Exp, scale=-0.5), reads=[self.b_rs[ri]], writes=[self.b_rs[ri]])
        return r

    def bc8(self, t2d_tile, ri, T):
        return bass.AP(self.rs, ri * TP, [[2 * TP, 128], [0, 8], [1, T]])

    def norm_pre(self, tl, l, which):
        fw, nc = self.fw, self.nc
        T = tl["T"]
        kA, kB = (0, 1) if which == "mix" else (3, 4)
        self.stats_sumsq(self.xT[:, :, 0:T], self.b_xT, T, 4)
        self.rstd_from(self.P[4][:, 0:T], [self.b_P[4]], T, 1.0 / D)
        r0 = self.rs[:, 0, 0:T]
        for c in range(8):
            if c < 5:
                fw.op(fw.dve, lambda c=c: nc.vector.tensor_tensor(out=self.tmp[:, c, 0:T], in0=self.xT[:, c, 0:T], in1=r0, op=ALU.mult),
                      reads=[self.b_xT[c], self.b_rs[0]], writes=[self.b_tmp[c]])
            else:
                fw.op(fw.pool, lambda c=c: nc.gpsimd.tensor_tensor(out=self.tmp[:, c, 0:T], in0=self.xT[:, c, 0:T], in1=r0, op=ALU.mult),
                      reads=[self.b_xT[c], self.b_rs[0]], writes=[self.b_tmp[c]])
        for c in [0, 5, 1, 6, 2, 7, 3, 4]:
            for sg in tl["segs"]:
                o, n, s = sg["off"], sg["n"], sg["seq"]
                if c >= 5:
                    fw.op(fw.dve, lambda c=c, o=o, n=n, s=s: nc.vector.tensor_scalar(self.hT[:, c, o:o + n], self.tmp[:, c, o:o + n],
                                                                                    self.cf(l, kA, c, s), self.cf(l, kB, c, s), ALU.mult, ALU.add),
                          reads=[self.b_tmp[c], self.b_coef], writes=[self.b_hT[c]])
                else:
                    fw.op(fw.act, lambda c=c, o=o, n=n, s=s: nc.scalar.activation(self.hT[:, c, o:o + n], self.tmp[:, c, o:o + n], AF.Identity,
                                                                                 bias=self.cf(l, kB, c, s), scale=self.cf(l, kA, c, s)),
                          reads=[self.b_tmp[c], self.b_coef], writes=[self.b_hT[c]])

    def post_norm_residual(self, tl, l, which):
        fw, nc = self.fw, self.nc
        T = tl["T"]
        kG = 2 if which == "mix" else 5
        self.stats_sumsq(self.tmp[:, :, 0:T], self.b_tmp, T, 4)
        self.rstd_from(self.P[4][:, 0:T], [self.b_P[4]], T, 1.0 / D)
        r0 = self.rs[:, 0, 0:T]
        for c in range(8):
            if c < 3:
                fw.op(fw.dve, lambda c=c: nc.vector.tensor_tensor(out=self.tmp[:, c, 0:T], in0=self.tmp[:, c, 0:T], in1=r0, op=ALU.mult),
                      reads=[self.b_tmp[c], self.b_rs[0]], writes=[self.b_tmp[c]])
            else:
                fw.op(fw.pool, lambda c=c: nc.gpsimd.tensor_tensor(out=self.tmp[:, c, 0:T], in0=self.tmp[:, c, 0:T], in1=r0, op=ALU.mult),
                      reads=[self.b_tmp[c], self.b_rs[0]], writes=[self.b_tmp[c]])
        for c in range(8):
            for sg in tl["segs"]:
                o, n, s = sg["off"], sg["n"], sg["seq"]
                fw.op(fw.dve, lambda c=c, o=o, n=n, s=s: nc.vector.scalar_tensor_tensor(
                    out=self.xT[:, c, o:o + n], in0=self.tmp[:, c, o:o + n], scalar=self.cf(l, kG, c, s), in1=self.xT[:, c, o:o + n],
                    op0=ALU.mult, op1=ALU.add), reads=[self.b_tmp[c], self.b_coef], writes=[self.b_xT[c]])

    def dense(self, P, pb, w, bw, col0, rhs_fn, rhs_bufs, KC, T, last_inc=True):
        fw, nc = self.fw, self.nc
        for k in range(KC):
            fw.op(fw.pe, lambda k=k: nc.tensor.matmul(P[:, 0:T], w[:, k, col0:col0 + 128], rhs_fn(k), start=(k == 0), stop=(k == KC - 1)),
                  reads=[bw, rhs_bufs[k]], writes=[self.b_P[pb]], inc=(k == KC - 1 and last_inc))

    def useg(self, tl):
        return [i * (CW - 1 + sg["n"]) for i, sg in enumerate(tl["segs"])]

    def layer0(self, tl):
        fw, nc = self.fw, self.nc
        T = tl["T"]
        H = CW - 1
        uo = self.useg(tl)
        if tl["kind"] == "s":
            for i, sg in enumerate(tl["segs"]):
                fw.dma(fw.sp, self.cin[0:H, :], self.cconv[sg["seq"] - 1], self.d_cin, writes=[self.b_cin])
                P = self.P[i % 4]
                for c in range(8):
                    fw.op(fw.pe, lambda c=c, P=P: nc.tensor.transpose(P[:, c * 32:c * 32 + H], self.cin[0:H, c * 128:(c + 1) * 128], self.ident[0:H, 0:H]),
                          reads=[self.b_cin, self.b_cst], writes=[self.b_P[i % 4]], inc=(c == 7))
                fw.op(fw.dve, lambda P=P, i=i: nc.vector.tensor_copy(self.u[:, :, uo[i]:uo[i] + H], P[:, 0:256].rearrange("p (c h) -> p c h", c=8)[:, :, 0:H]),
                      reads=[self.b_P[i % 4]], writes=self.b_u)
        self.norm_pre(tl, 0, "mix")
        for b in range(4):
            w, bw = self.wnext("pw1")
            for jj in range(2):
                j = 2 * b + jj
                pa, pg = (0, 1) if j % 2 == 0 else (2, 3)
                self.dense(self.P[pa], pa, w, bw, jj * 128, lambda k: self.hT[:, k, 0:T], self.b_hT, 8, T)
                self.dense(self.P[pg], pg, w, bw, 256 + jj * 128, lambda k: self.hT[:, k, 0:T], self.b_hT, 8, T)
                sgi = j % 2
                fw.op(fw.act, lambda pg=pg, j=j, sgi=sgi: nc.scalar.activation(self.sg[sgi][:, 0:T], self.P[pg][:, 0:T], AF.Sigmoid, bias=self.V("b_pw1", 8 + j)),
                      reads=[self.b_P[pg], self.b_vecs], writes=[self.b_sg[sgi]])
                for i, sg in enumerate(tl["segs"]):
                    o, n = sg["off"], sg["n"]
                    fw.op(fw.dve, lambda pa=pa, j=j, sgi=sgi, o=o, n=n, i=i: nc.vector.scalar_tensor_tensor(
                        out=self.u[:, j, uo[i] + H:uo[i] + H + n], in0=self.P[pa][:, o:o + n], scalar=self.V("b_pw1", j), in1=self.sg[sgi][:, o:o + n],
                        op0=ALU.add, op1=ALU.mult), reads=[self.b_P[pa], self.b_sg[sgi], self.b_vecs], writes=[self.b_u[j]])
                    if sg["last"]:
                        fw.op(fw.dve, lambda pa=pa, j=j, sgi=sgi, o=o, n=n, i=i: nc.vector.scalar_tensor_tensor(
                            out=self.ucf[:, i, j, :], in0=self.P[pa][:, o + n - H:o + n], scalar=self.V("b_pw1", j), in1=self.sg[sgi][:, o + n - H:o + n],
                            op0=ALU.add, op1=ALU.mult), reads=[self.b_P[pa], self.b_sg[sgi], self.b_vecs], writes=[self.b_ucf[i]])
        for i, sg in enumerate(tl["segs"]):
            if sg["last"]:
                dst = self.o_ccp if tl["kind"] == "p" else self.o_ccs[sg["seq"] - 1]
                self.emit_rows_out(self.ucf[:, i], [self.b_ucf[i]], 8, H, dst)
        nseg = len(tl["segs"])
        nn = tl["segs"][0]["n"]
        UWt = self.u[:].ap[0][0]
        UWc = CW - 1 + TP
        for c in range(8):
            pb = c % 4
            for k in range(CW):
                slot = (c * CW + k) % NDG
                wv = self.V("w_dw", c * CW + k)
                fw.op(fw.dve, lambda slot=slot, wv=wv: nc.vector.tensor_scalar(self.dg[:, slot, :], self.ident_bf, wv, None, ALU.mult),
                      reads=[self.b_cbf, self.b_vecs], writes=[self.b_dg[slot]])
                rhs = bass.AP(self.u, c * UWc + k, [[UWt, 128], [CW - 1 + nn, nseg], [1, nn]])
                fw.op(fw.pe, lambda slot=slot, rhs=rhs, pb=pb, k=k: nc.tensor.matmul(self.P[pb][:, 0:T], self.dg[:, slot, :], rhs, start=(k == 0), stop=(k == CW - 1)),
                      reads=[self.b_dg[slot], self.b_u[c]], writes=[self.b_P[pb]], inc=True)
            fw.op(fw.act, lambda c=c, pb=pb: nc.scalar.activation(self.tmp[:, c, 0:T], self.P[pb][:, 0:T], AF.Identity, bias=self.V("b_dw", c)),
                  reads=[self.b_P[pb], self.b_vecs], writes=[self.b_tmp[c]])
        if tl["kind"] == "p":
            fw.op(fw.dve, lambda: nc.vector.tensor_copy(self.u[:, :, 0:H], self.u[:, :, T:T + H]), reads=self.b_u, writes=self.b_u)
        fw.op(fw.dve, lambda: nc.vector.tensor_copy(self.hT[:, :, 0:T], self.tmp[:, :, 0:T]), reads=self.b_tmp, writes=self.b_hT)
        for c in range(8):
            fw.op(fw.pe, lambda c=c: nc.tensor.matmul(self.P[5][:, 0:T], self.ones_bf, self.hT[:, c, 0:T], start=(c == 0), stop=(c == 7)),
                  reads=[self.b_hT[c], self.b_cbf], writes=[self.b_P[5]], inc=(c == 7))
        self.stats_sumsq(self.tmp[:, :, 0:T], self.b_tmp, T, 4)
        mu = self.rs[:, 1, 0:T]
        fw.op(fw.dve, lambda: nc.vector.tensor_scalar(mu, self.P[5][:, 0:T], 1.0 / D, None, ALU.mult), reads=[self.b_P[5]], writes=[self.b_rs[1]])
        v0 = self.sg[0][:, 0:T]
        fw.op(fw.dve, lambda: nc.vector.tensor_tensor(out=v0, in0=mu, in1=mu, op=ALU.mult), reads=[self.b_rs[1]], writes=[self.b_sg[0]])
        fw.op(fw.dve, lambda: nc.vector.scalar_tensor_tensor(out=v0, in0=self.P[4][:, 0:T], scalar=1.0 / D, in1=v0, op0=ALU.mult, op1=ALU.subtract),
              reads=[self.b_P[4], self.b_sg[0]], writes=[self.b_sg[0]])
        self.rstd_from(v0, [self.b_sg[0]], T, 1.0)
        fw.op(fw.dve, lambda: nc.vector.tensor_tensor(out=self.tmp[:, :, 0:T], in0=self.tmp[:, :, 0:T], in1=self.bc8(self.rs, 1, T), op=ALU.subtract),
              reads=self.b_tmp + [self.b_rs[1]], writes=self.b_tmp)
        fw.op(fw.dve, lambda: nc.vector.tensor_tensor(out=self.tmp[:, :, 0:T], in0=self.tmp[:, :, 0:T], in1=self.bc8(self.rs, 0, T), op=ALU.mult),
              reads=self.b_tmp + [self.b_rs[0]], writes=self.b_tmp)
        for c in range(8):
            fw.op(fw.act, lambda c=c: nc.scalar.activation(self.hT[:, c, 0:T], self.tmp[:, c, 0:T], AF.Silu, bias=self.V("ln_b", c), scale=self.V("ln_g", c)),
                  reads=[self.b_tmp[c], self.b_vecs], writes=[self.b_hT[c]])
        for b in range(2):
            w, bw = self.wnext("pw2")
            for jj in range(4):
                j = 4 * b + jj
                pb = j % 4
                self.dense(self.P[pb], pb, w, bw, jj * 128, lambda k: self.hT[:, k, 0:T], self.b_hT, 8, T)
                fw.op(fw.act, lambda j=j, pb=pb: nc.scalar.activation(self.tmp[:, j, 0:T], self.P[pb][:, 0:T], AF.Identity, bias=self.V("b_pw2", j)),
                      reads=[self.b_P[pb], self.b_vecs], writes=[self.b_tmp[j]])
        self.post_norm_residual(tl, 0, "mix")

    def emit_rows_out(self, src_tile, src_bufs, nchunk, nrow, dst_ap):
        fw, nc = self.fw, self.nc
        for c0 in range(0, nchunk, 4):
            pb = 6
            P = self.P[pb]
            nn = min(4, nchunk - c0)
            for cc in range(nn):
                fw.op(fw.pe, lambda cc=cc, c0=c0: nc.tensor.transpose(P[0:nrow, cc * 128:(cc + 1) * 128], src_tile[:, c0 + cc, 0:nrow], self.ident),
                      reads=list(src_bufs) + [self.b_cst], writes=[self.b_P[pb]], inc=(cc == nn - 1))
            g0 = (c0 // 8) * 8
            fw.op(fw.act, lambda c0=c0, nn=nn, g0=g0: nc.scalar.copy(self.cin[0:nrow, (c0 - g0) * 128:(c0 - g0 + nn) * 128], P[0:nrow, 0:nn * 128]),
                  reads=[self.b_P[pb]], writes=[self.b_cin])
            if c0 + nn >= min(nchunk, g0 + 8):
                ob = Buf()
                fw.dma(fw.pool, dst_ap[:, g0 * 128:(c0 + nn) * 128], self.cin[0:nrow, 0:(c0 + nn - g0) * 128], self.d_cin, reads=[self.b_cin], writes=[ob])
                self.outbufs.append(ob)

    def ffn(self, tl, l):
        fw, nc = self.fw, self.nc
        T = tl["T"]
        self.norm_pre(tl, l, "ffn")
        for b in range(11):
            w, bw = self.wnext(f"gu{l}")
            for jj in range(2):
                j = 2 * b + jj
                pa, pg = (0, 1) if j % 2 == 0 else (2, 3)
                self.dense(self.P[pa], pa, w, bw, jj * 128, lambda k: self.hT[:, k, 0:T], self.b_hT, 8, T)
                self.dense(self.P[pg], pg, w, bw, 256 + jj * 128, lambda k: self.hT[:, k, 0:T], self.b_hT, 8, T)
                sgi = j % 2
                fw.op(fw.act, lambda pa=pa, sgi=sgi: nc.scalar.activation(self.sg[sgi][:, 0:T], self.P[pa][:, 0:T], AF.Silu),
                      reads=[self.b_P[pa]], writes=[self.b_sg[sgi]])
                fw.op(fw.dve, lambda pg=pg, sgi=sgi, j=j: nc.vector.tensor_tensor(out=self.CH[:, j, 0:T], in0=self.sg[sgi][:, 0:T], in1=self.P[pg][:, 0:T], op=ALU.mult),
                      reads=[self.b_P[pg], self.b_sg[sgi]], writes=[self.b_CH[j]])
        for b in range(8):
            w, bw = self.wnext(f"dn{l}")
            pb = b % 4
            self.dense(self.P[pb], pb, w, bw, 0, lambda k: self.CH[:, k, 0:T], self.b_CH, 22, T)
            if b % 2 == 0:
                fw.op(fw.act, lambda b=b, pb=pb: nc.scalar.copy(self.tmp[:, b, 0:T], self.P[pb][:, 0:T]), reads=[self.b_P[pb]], writes=[self.b_tmp[b]])
            else:
                fw.op(fw.dve, lambda b=b, pb=pb: nc.vector.tensor_copy(self.tmp[:, b, 0:T], self.P[pb][:, 0:T]), reads=[self.b_P[pb]], writes=[self.b_tmp[b]])
        self.post_norm_residual(tl, l, "ffn")

    def gdn_alloc(self):
        fw = self.fw
        sb = fw.sb
        G = 4
        CA = self.CA = 128
        LA = self.LA = 7
        self.ab = sb("ab", [128, 8, 32], F32)
        self.gt = sb("gt", [128, 8, 16], F32)
        self.beta = sb("beta", [128, 8, 16], F32)
        self.gc = sb("gc", [128, 8, 16], F32)
        self.egc = sb("egc", [128, 8, 16], F32)
        self.t16 = [sb(f"t16{i}", [128, 8, 16], F32) for i in range(3)]
        self.g3 = sb("g3", [128, 3, 8, 16], BF16)
        fl = lambda n, k, dt: sb(n, [128, k], dt)
        self.QKDs = [fl(f"QKD{i}", G * CA, BF16) for i in range(4)]
        self.ATs = [fl(f"AT{i}", G * CA, BF16) for i in range(4)]
        self.dlasts = [sb(f"dlast{i}", [128, G], F32) for i in range(4)]
        self.gtots = [sb(f"gtot{i}", [128, G], F32) for i in range(4)]
        self.masks = fl("masks", 2 * LA * CA, BF16)
        self.b_masks = Buf()
        self.pa = []
        for si in range(2):
            d = {}
            bb = {}
            def mk(name, n, dt, alias=None, abuf=None):
                if alias is not None:
                    d[name] = alias
                    bb[name] = abuf
                else:
                    d[name] = fl(f"{name}{si}", n, dt)
                    bb[name] = Buf(f"{name}{si}")
            mk("rhsU", 3 * G * CA, BF16)
            if si == 0:
                mk("rhsB", G * CA, BF16)
                mk("DT", G * CA, F32)
                mk("Bm", G * CA, F32)
                mk("X0", G * CA, BF16)
            else:
                mk("rhsB", 0, BF16, alias=self.sq1[0], abuf=self.b_sq1[0])
                mk("DT", 0, F32, alias=self.qkf[0], abuf=self.b_qkf[0])
                mk("Bm", 0, F32, alias=self.qkf[1], abuf=self.b_qkf[1])
                mk("X0", 0, BF16, alias=self.sq1[1], abuf=self.b_sq1[1])
            mk("Y0", G * CA, BF16)
            for nm in ("A0", "A1", "B0", "B1", "Q0", "Q1", "No0", "No1", "Mo0", "Mo1"):
                mk(nm, G * CA, BF16)
            self.pa.append((d, bb))
        self.Sbf = fl("Sbf", G * HD, BF16)
        self.rr = fl("rr", G * HD, BF16)
        self.vn = fl("vn", G * HD, BF16)
        self.vnd = fl("vnd", G * HD, BF16)
        self.oo = fl("oo", G * HD, F32)
        self.osq = fl("osq", G * HD, F32)
        self.t1 = self.oo
        self.t2 = self.osq
        self.oss = sb("oss", [128, 2, G], F32)
        self.ktok = fl("ktok", 2 * HD, BF16)
        n = ["ab", "gt", "beta", "gc", "egc", "t160", "t161", "t162", "g3",
             "Sbf", "AT0", "AT1", "AT2", "AT3", "QKD0", "QKD1", "QKD2", "QKD3", "dl0", "dl1", "dl2", "dl3", "rr", "vn", "vnd", "oo", "osq", "oss", "ktok"]
        self.gb = {k: Buf(k) for k in n}
        self.gb["t1"] = self.gb["oo"]
        self.gb["t2"] = self.gb["osq"]

    def gdn(self, tl):
        fw, nc = self.fw, self.nc
        T = tl["T"]
        C = tl["C"]
        G = 4
        gb = self.gb
        self.norm_pre(tl, 1, "mix")
        nch = T // C
        chunks = []
        for sg in tl["segs"]:
            for q in range(sg["n"] // C):
                chunks.append((sg["off"] + q * C, sg, q == 0, q == sg["n"] // C - 1))
        w, bw = self.wnext("gab")
        Pab = self.P[6]
        for n_, (co, sg, _, _) in enumerate(chunks):
            for k in range(8):
                fw.op(fw.pe, lambda n_=n_, k=k, co=co: nc.tensor.matmul(Pab[0:C, n_ * 32:(n_ + 1) * 32], self.hT[:, k, co:co + C], w[:, k, 0:32], start=(k == 0), stop=(k == 7)),
                      reads=[bw, self.b_hT[k]], writes=[self.b_P[6]], inc=(k == 7 and n_ == nch - 1))
        abv = Pab[0:C, 0:nch * 32].rearrange("p (n j) -> p n j", j=32)
        A16 = lambda t: t[0:C, 0:nch, :]
        fw.op(fw.act, lambda: nc.scalar.activation(A16(self.beta), abv[:, :, 16:32], AF.Sigmoid), reads=[self.b_P[6]], writes=[gb["beta"]])
        xx, ax, ee = A16(self.t16[0]), A16(self.t16[1]), A16(self.t16[2])
        dtb = bass.AP(self.hrow, 16, [[32, C], [0, nch], [1, 16]])
        nA = bass.AP(self.hrow, 0, [[32, C], [0, nch], [1, 16]])
        fw.op(fw.dve, lambda: nc.vector.tensor_tensor(out=xx, in0=abv[:, :, 0:16], in1=dtb, op=ALU.add), reads=[self.b_P[6], self.b_hrow], writes=[gb["t160"]])
        fw.op(fw.act, lambda: nc.scalar.activation(ax, xx, AF.Abs), reads=[gb["t160"]], writes=[gb["t161"]])
        fw.op(fw.act, lambda: nc.scalar.activation(ee, ax, AF.Exp, scale=-1.0), reads=[gb["t161"]], writes=[gb["t162"]])
        fw.op(fw.act, lambda: nc.scalar.activation(ee, ee, AF.Ln, bias=self.sc[0:C, 1:2]), reads=[gb["t162"], self.b_sc], writes=[gb["t162"]])
        fw.op(fw.dve, lambda: nc.vector.scalar_tensor_tensor(out=xx, in0=xx, scalar=0.0, in1=ee, op0=ALU.max, op1=ALU.add),
              reads=[gb["t160"], gb["t162"]], writes=[gb["t160"]])
        fw.op(fw.dve, lambda: nc.vector.tensor_tensor(out=A16(self.gt), in0=xx, in1=nA, op=ALU.mult), reads=[gb["t160"], self.b_hrow], writes=[gb["gt"]])
        Pg = self.P[5]
        Ub = self.cbf[0:C, 256:256 + C]
        G3 = lambda k: self.g3[0:C, k, 0:nch, :]
        r1, r2 = A16(self.t16[1]), A16(self.t16[2])
        fw.op(fw.dve, lambda: nc.vector.tensor_copy(G3(0), A16(self.gt)), reads=[gb["gt"]], writes=[gb["g3"]])
        fw.op(fw.dve, lambda: nc.vector.tensor_tensor(out=r1, in0=A16(self.gt), in1=G3(0), op=ALU.subtract), reads=[gb["gt"], gb["g3"]], writes=[gb["t161"]])
        fw.op(fw.dve, lambda: nc.vector.tensor_copy(G3(1), r1), reads=[gb["t161"]], writes=[gb["g3"]])
        fw.op(fw.dve, lambda: nc.vector.tensor_tensor(out=r2, in0=r1, in1=G3(1), op=ALU.subtract), reads=[gb["t161"], gb["g3"]], writes=[gb["t162"]])
        fw.op(fw.dve, lambda: nc.vector.tensor_copy(G3(2), r2), reads=[gb["t162"]], writes=[gb["g3"]])
        for n_ in range(nch):
            for k in range(3):
                fw.op(fw.pe, lambda n_=n_, k=k: nc.tensor.matmul(Pg[0:C, n_ * 16:(n_ + 1) * 16], Ub, self.g3[0:C, k, n_, :], start=(k == 0), stop=(k == 2)),
                      reads=[gb["g3"], self.b_cbf], writes=[self.b_P[5]], inc=(n_ == nch - 1 and k == 2))
        fw.op(fw.dve, lambda: nc.vector.tensor_copy(A16(self.gc), Pg[0:C, 0:nch * 16].rearrange("p (n j) -> p n j", j=16)), reads=[self.b_P[5]], writes=[gb["gc"]])
        fw.op(fw.act, lambda: nc.scalar.activation(A16(self.egc), A16(self.gc), AF.Exp), reads=[gb["gc"]], writes=[gb["egc"]])

        if self.stop == "gdn_ab":
            raise StopBuild()
        for g in range(G):
            self.gdn_group(tl, g, chunks)
        for b in range(8):
            w, bw = self.wnext("gout")
            pb = b % 4
            self.dense(self.P[pb], pb, w, bw, 0, lambda k: self.CH[:, k, 0:T], self.b_CH, 16, T)
            if b % 2 == 0:
                fw.op(fw.act, lambda b=b, pb=pb: nc.scalar.copy(self.tmp[:, b, 0:T], self.P[pb][:, 0:T]), reads=[self.b_P[pb]], writes=[self.b_tmp[b]])
            else:
                fw.op(fw.dve, lambda b=b, pb=pb: nc.vector.tensor_copy(self.tmp[:, b, 0:T], self.P[pb][:, 0:T]), reads=[self.b_P[pb]], writes=[self.b_tmp[b]])
        self.post_norm_residual(tl, 1, "mix")
        for sg in tl["segs"]:
            if sg["last"]:
                hs = sg["seq"]
                dst = self.o_gcp if tl["kind"] == "p" else self.o_gcs[hs - 1]
                self.emit_rows_out(self.ghalo[:, hs], self.b_ghalo[hs], 32, SCW - 1, dst)

    def gdn_group(self, tl, g, chunks):
        fw, nc = self.fw, self.nc
        T = tl["T"]
        C = tl["C"]
        G = 4
        gb = self.gb
        HL = SCW - 1
        QC, KC_, VC_, ZC = 16, 18, 20, 24
        segs = tl["segs"]
        so = [i * (HL + sg["n"]) for i, sg in enumerate(segs)]
        cnt = 0
        for bi in range(3):
            w, bw = self.wnext("gin")
            pend = []
            for jj in range(4):
                pb = cnt % 4
                cnt += 1
                P = self.P[pb]
                self.dense(P, pb, w, bw, jj * 128, lambda k: self.hT[:, k, 0:T], self.b_hT, 8, T)
                if bi == 2:
                    dch = ZC + jj
                    fw.op(fw.act, lambda P=P, dch=dch: nc.scalar.activation(self.CH[:, dch, 0:T], P[:, 0:T], AF.Silu), reads=[self.b_P[pb]], writes=[self.b_CH[dch]])
                    continue
                if bi == 0:
                    cch = (2 * g + jj) if jj < 2 else (8 + 2 * g + jj - 2)
                    dch = (QC + jj) if jj < 2 else (KC_ + jj - 2)
                else:
                    cch = 16 + 4 * g + jj
                    dch = VC_ + jj
                si = cnt % 2
                bst = self.b_cstg[si]
                stg = self.cstg[si][:].bitcast(BF16)
                pst_stg = stg.ap[0][0]
                pcv = 5 + si
                Pc = self.P[pcv]
                nseg = len(segs)
                nn = segs[0]["n"]
                for i, sg in enumerate(segs):
                    hs = sg["seq"]
                    o, n = sg["off"], sg["n"]
                    if tl["kind"] == "s" and g == 0 and bi == 0 and jj == 0:
                        self.load_ghalo(sg)
                    fw.op(fw.act, lambda P=P, stg=stg, i=i, o=o, n=n: nc.scalar.copy(stg[:, so[i] + HL:so[i] + HL + n], P[:, o:o + n]),
                          reads=[self.b_P[pb]], writes=[bst])
                    fw.op(fw.dve, lambda stg=stg, i=i, hs=hs, cch=cch: nc.vector.tensor_copy(stg[:, so[i]:so[i] + HL], self.ghalo[:, hs, cch, :]),
                          reads=[self.b_ghalo[hs][cch]], writes=[bst])
                    fw.op(fw.dve, lambda P=P, hs=hs, cch=cch, o=o, n=n: nc.vector.tensor_copy(self.ghalo[:, hs, cch, :], P[:, o + n - HL:o + n]),
                          reads=[self.b_P[pb]], writes=[self.b_ghalo[hs][cch]])
                dgv = self.sg[1][:].bitcast(BF16).rearrange("p (s c) -> p s c", s=8)
                for k in range(SCW):
                    slot = (self.dg2i) % 8
                    self.dg2i += 1
                    wv = self.V("w_gc", cch * SCW + k)
                    fw.op(fw.dve, lambda slot=slot, wv=wv: nc.vector.tensor_scalar(dgv[:, slot, :], self.ident_bf, wv, None, ALU.mult),
                          reads=[self.b_cbf, self.b_vecs], writes=[self.b_dg2[slot]])
                    rhs = stg[:, 0:nseg * (HL + nn)].rearrange("p (s w) -> p s w", s=nseg)[:, :, k:k + nn]
                    fw.op(fw.pe, lambda slot=slot, rhs=rhs, k=k: nc.tensor.matmul(Pc[:, 0:T], dgv[:, slot, :], rhs, start=(k == 0), stop=(k == SCW - 1)),
                          reads=[self.b_dg2[slot], bst], writes=[self.b_P[pcv]], inc=True)
                cvo, bcv = Pc, self.b_P[pcv]
                if bi == 1:
                    fw.op(fw.act, lambda cvo=cvo, dch=dch: nc.scalar.activation(self.CH[:, dch, 0:T], cvo[:, 0:T], AF.Silu), reads=[bcv], writes=[self.b_CH[dch]])
                else:
                    qf, bqf = [(self.qkf[0][:, 0:T], self.b_qkf[0]), (self.qkf[1][:, 0:T], self.b_qkf[1]),
                               (self.sg[0][:, 0:T], self.b_sg[0]), (self.rs[:, 1, 0:T], self.b_rs[1])][jj]
                    fw.op(fw.act, lambda cvo=cvo, qf=qf: nc.scalar.activation(qf, cvo[:, 0:T], AF.Silu), reads=[bcv], writes=[bqf])
                    pend.append((qf, bqf, dch, jj, si))
            for (qf, bqf, dch, jj, si) in (pend if bi == 0 else []):
                    s1, bs1 = self.sq1[si], self.b_sq1[si]
                    fw.op(fw.act, lambda qf=qf, s1=s1: nc.scalar.activation(s1[:, 0:T], qf, AF.Square), reads=[bqf], writes=[bs1])
                    fw.op(fw.pe, lambda s1=s1: nc.tensor.matmul(self.P[4][:, 0:T], self.ones_bf, s1[:, 0:T], start=True, stop=True),
                          reads=[bs1, self.b_cbf], writes=[self.b_P[4]])
                    r = self.rstd_from(self.P[4][:, 0:T], [self.b_P[4]], T, 1.0)
                    if jj < 2:
                        fw.op(fw.dve, lambda qf=qf, dch=dch, r=r: nc.vector.scalar_tensor_tensor(
                            out=self.CH[:, dch, 0:T], in0=qf, scalar=float(HD ** -0.5), in1=r, op0=ALU.mult, op1=ALU.mult),
                            reads=[bqf, self.b_rs[0]], writes=[self.b_CH[dch]])
                    else:
                        fw.op(fw.dve, lambda qf=qf, dch=dch, r=r: nc.vector.tensor_tensor(out=self.CH[:, dch, 0:T], in0=qf, in1=r, op=ALU.mult),
                              reads=[bqf, self.b_rs[0]], writes=[self.b_CH[dch]])
        if self.stop == "gdn_proj":
            raise StopBuild()
        def run(gens):
            gens = list(gens)
            while gens:
                for gen in list(gens):
                    try:
                        next(gen)
                    except StopIteration:
                        gens.remove(gen)
        N = len(chunks)
        act_gens = {}
        doneA, doneB = set(), set()
        nextA, nextB = 0, 0
        while len(doneB) < N:
            while nextA < N and sum(1 for k in act_gens if k[0] == "A") < NA_INFLIGHT and (nextA < 4 or (nextA - 4) in doneB):
                act_gens[("A", nextA)] = self.gdn_phaseA(tl, g, nextA, chunks[nextA])
                nextA += 1
            if nextB < N and nextB in doneA and not any(k[0] == "B" for k in act_gens):
                act_gens[("B", nextB)] = self.gdn_phaseB(tl, g, nextB, chunks[nextB])
                nextB += 1
            for key in list(act_gens):
                try:
                    if SEQ_EMIT:
                        for _ in act_gens[key]:
                            pass
                        raise StopIteration
                    next(act_gens[key])
                except StopIteration:
                    del act_gens[key]
                    (doneA if key[0] == "A" else doneB).add(key[1])

    def _gviews(self, tl, g, n_):
        C = tl["C"]
        G = 4
        v = dict(
            gv=lambda t, off=0: t[0:C, off:off + G * C].rearrange("p (g c) -> p g c", g=G),
            g2=lambda t, off=0: t[0:C, off:off + G * C],
            hv=lambda t: t[0:C, 0:G * HD].rearrange("p (g v) -> p g v", g=G),
            h2=lambda t: t[0:C, 0:G * HD],
            hsl=lambda t, hh, w=None: t[0:C, hh * (w or C):(hh + 1) * (w or C)],
            bc_mask=lambda base: bass.AP(self.cst, base, [[640, C], [0, G], [1, C]]),
            colb=lambda t, m: bass.AP(t, n_ * 16 + 4 * g, [[128, C], [1, G], [0, m]]),
            psv=lambda Pt, w=None: Pt[0:C, 0:G * (w or C)].rearrange("p (g c) -> p g c", g=G),
        )
        return v

    def gdn_phaseA(self, tl, g, n_, chunk):
        fw, nc = self.fw, self.nc
        co, sg, cfirst, clast = chunk
        C = tl["C"]
        G = 4
        CA, LA = self.CA, self.LA
        L = int(np.log2(C))
        gb = self.gb
        P, bP = self.P, self.b_P
        T_, Bf = self.pa[n_ % 2]
        par = n_ % 4
        pa, pb_ = (2, 6) if n_ % 2 == 0 else (0, 1)
        pc = pa
        V = self._gviews(tl, g, n_)
        gv, g2, hsl, bc_mask, colb, psv = V["gv"], V["g2"], V["hsl"], V["bc_mask"], V["colb"], V["psv"]
        ATp, QKDp = self.ATs[par], self.QKDs[par]
        bATp, bQKDp, bdl = gb[f"AT{par}"], gb[f"QKD{par}"], gb[f"dl{par}"]
        dlast, gtot = self.dlasts[par], self.gtots[par]
        rhsU, rhsB, DT, Bm, X0, Y0 = T_["rhsU"], T_["rhsB"], T_["DT"], T_["Bm"], T_["X0"], T_["Y0"]
        for k in range(3):
            g3b = bass.AP(self.g3, k * 128 + n_ * 16 + 4 * g, [[384, C], [1, G], [0, C]])
            fw.op(fw.dve, lambda k=k, g3b=g3b: nc.vector.tensor_tensor(out=gv(rhsU, k * G * C), in0=bass.AP(self.cbf, 256, [[384, C], [0, G], [1, C]]), in1=g3b, op=ALU.mult),
                  reads=[self.b_cbf, gb["g3"]], writes=[Bf["rhsU"]])
        fw.op(fw.dve, lambda: nc.vector.tensor_tensor(out=gv(rhsB), in0=bc_mask(0), in1=colb(self.beta, C), op=ALU.mult),
              reads=[self.b_cst, gb["beta"]], writes=[Bf["rhsB"]])
        yield
        for k in range(3):
            fw.op(fw.pe, lambda k=k: nc.tensor.matmul(P[pa][:, 0:G * C], self.cbf[0:C, 128:256], g2(rhsU, k * G * C), start=(k == 0), stop=(k == 2)),
                  reads=[Bf["rhsU"], self.b_cbf], writes=[bP[pa]], inc=(k == 2))
        fw.op(fw.pe, lambda: nc.tensor.matmul(P[pb_][0:C, 0:G * C], self.cbf[0:C, 128:128 + C], g2(rhsB), start=True, stop=True),
              reads=[Bf["rhsB"], self.b_cbf], writes=[bP[pb_]])
        yield
        gcrow = P[pa][:, 0:G * C].rearrange("p (g c) -> p g c", g=G)
        brow = psv(P[pb_])
        fw.op(fw.dve, lambda: nc.vector.tensor_tensor(out=gv(DT), in0=gcrow[0:C], in1=colb(self.gc, C), op=ALU.subtract),
              reads=[bP[pa], gb["gc"]], writes=[Bf["DT"]])
        fw.op(fw.act, lambda: nc.scalar.activation(dlast[0:C, :], gv(DT)[:, :, C - 1], AF.Exp), reads=[Bf["DT"]], writes=[bdl])
        fw.op(fw.act, lambda: nc.scalar.activation(gtot[:, :], gcrow[:, :, C - 1], AF.Exp), reads=[bP[pa]], writes=[bdl])
        fw.op(fw.dve, lambda: nc.vector.scalar_tensor_tensor(out=gv(DT), in0=gv(DT), scalar=0.0, in1=bc_mask(384), op0=ALU.min, op1=ALU.add),
              reads=[Bf["DT"], self.b_cst], writes=[Bf["DT"]])
        yield
        for jj in range(2):
            kT = self.CH[:, 18 + jj, co:co + C]
            qT = self.CH[:, 16 + jj, co:co + C]
            fw.op(fw.pe, lambda jj=jj, kT=kT: nc.tensor.matmul(P[pc][0:C, (jj * 2) * C:(jj * 2 + 1) * C], kT, kT, start=True, stop=True),
                  reads=[self.b_CH[18 + jj]], writes=[bP[pc]], inc=False)
            fw.op(fw.pe, lambda jj=jj, kT=kT, qT=qT: nc.tensor.matmul(P[pc][0:C, (jj * 2 + 1) * C:(jj * 2 + 2) * C], kT, qT, start=True, stop=True),
                  reads=[self.b_CH[18 + jj], self.b_CH[16 + jj]], writes=[bP[pc]], inc=(jj == 1))
        fw.op(fw.act, lambda: nc.scalar.activation(g2(DT), g2(DT), AF.Exp), reads=[Bf["DT"]], writes=[Bf["DT"]])
        yield
        kk_b = bass.AP(P[pc], 0, [[512, C], [2 * C, 2], [0, 2], [1, C]])
        kq_b = bass.AP(P[pc], C, [[512, C], [2 * C, 2], [0, 2], [1, C]])
        v4 = lambda t: t[0:C, 0:G * C].rearrange("p (a e c) -> p a e c", a=2, e=2)
        fw.op(fw.dve, lambda: nc.vector.tensor_tensor(out=v4(QKDp), in0=kq_b, in1=v4(DT), op=ALU.mult), reads=[bP[pc], Bf["DT"]], writes=[bQKDp])
        fw.op(fw.dve, lambda: nc.vector.tensor_tensor(out=gv(DT), in0=gv(DT), in1=bc_mask(512), op=ALU.mult), reads=[Bf["DT"], self.b_cst], writes=[Bf["DT"]])
        yield
        fw.op(fw.dve, lambda: nc.vector.tensor_tensor(out=gv(Bm), in0=brow, in1=gv(DT), op=ALU.mult), reads=[bP[pb_], Bf["DT"]], writes=[Bf["Bm"]])
        for a in range(2):
            kka = bass.AP(P[pc], a * 2 * C, [[512, C], [0, 2], [1, C]])
            sl = lambda t, a=a: t[0:C, a * 2 * C:(a + 1) * 2 * C].rearrange("p (e c) -> p e c", e=2)
            fw.op(fw.dve, lambda kka=kka, sl=sl: nc.vector.scalar_tensor_tensor(out=sl(X0), in0=kka, scalar=-1.0, in1=sl(Bm), op0=ALU.mult, op1=ALU.mult),
                  reads=[bP[pc], Bf["Bm"]], writes=[Bf["X0"]])
        for hh in range(G):
            fw.op(fw.pe, lambda hh=hh: nc.tensor.transpose(self.PB[0:C, hh * C:(hh + 1) * C], hsl(X0, hh), self.cbf[0:C, 0:C]),
                  reads=[Bf["X0"], self.b_cbf], writes=[self.b_PB], inc=(hh == G - 1))
        mkl = lambda which, lv: bass.AP(self.masks, which * LA * CA + lv * CA, [[2 * LA * CA, C], [0, G], [1, C]])
        idb = bass.AP(self.cbf, 0, [[384, C], [0, G], [1, C]])
        A = [T_["A0"], T_["A1"]]
        B = [T_["B0"], T_["B1"]]
        Q = [T_["Q0"], T_["Q1"]]
        No = [T_["No0"], T_["No1"]]
        Mo = [T_["Mo0"], T_["Mo1"]]
        nA, nB, nQ, nNo, nMo = ["A0", "A1"], ["B0", "B1"], ["Q0", "Q1"], ["No0", "No1"], ["Mo0", "Mo1"]
        fw.op(fw.act, lambda: nc.scalar.copy(g2(Y0), self.PB[0:C, 0:G * C]), reads=[self.b_PB], writes=[Bf["Y0"]])
        yield
        fw.op(fw.dve, lambda: nc.vector.tensor_tensor(out=gv(No[0]), in0=gv(X0), in1=mkl(0, 0), op=ALU.mult), reads=[Bf["X0"], self.b_masks], writes=[Bf["No0"]])
        fw.op(fw.dve, lambda: nc.vector.tensor_tensor(out=gv(B[1]), in0=gv(No[0]), in1=idb, op=ALU.add), reads=[Bf["No0"], self.b_cbf], writes=[Bf["B1"]])
        yield
        fw.op(fw.dve, lambda: nc.vector.tensor_tensor(out=gv(Mo[0]), in0=gv(Y0), in1=mkl(1, 0), op=ALU.mult), reads=[Bf["Y0"], self.b_masks], writes=[Bf["Mo0"]])
        fw.op(fw.dve, lambda: nc.vector.tensor_tensor(out=gv(A[1]), in0=gv(Mo[0]), in1=idb, op=ALU.add), reads=[Bf["Mo0"], self.b_cbf], writes=[Bf["A1"]])
        fw.op(fw.dve, lambda: nc.vector.tensor_tensor(out=gv(No[1]), in0=gv(X0), in1=mkl(0, 1), op=ALU.mult), reads=[Bf["X0"], self.b_masks], writes=[Bf["No1"]])
        fw.op(fw.dve, lambda: nc.vector.tensor_tensor(out=gv(Mo[1]), in0=gv(Y0), in1=mkl(1, 1), op=ALU.mult), reads=[Bf["Y0"], self.b_masks], writes=[Bf["Mo1"]])
        yield
        cur = 1
        for lv in range(1, L):
            nx = 1 - cur
            mp = lv % 2
            last = (lv == L - 1)
            if not last:
                for hh in range(G):
                    fw.op(fw.pe, lambda hh=hh, cur=cur, mp=mp: nc.tensor.matmul(P[pa][0:C, hh * C:(hh + 1) * C], hsl(No[mp], hh), hsl(A[cur], hh), start=True, stop=True),
                          reads=[Bf[nNo[mp]], Bf[nA[cur]]], writes=[bP[pa]], inc=(hh == G - 1))
            for hh in range(G):
                fw.op(fw.pe, lambda hh=hh, cur=cur, mp=mp: nc.tensor.matmul(P[pb_][0:C, hh * C:(hh + 1) * C], hsl(Mo[mp], hh), hsl(B[cur], hh), start=True, stop=True),
                      reads=[Bf[nMo[mp]], Bf[nB[cur]]], writes=[bP[pb_]], inc=(hh == G - 1))
            if lv + 1 < L:
                m2 = (lv + 1) % 2
                if lv + 1 < L - 1:
                    fw.op(fw.pool, lambda m2=m2, lv=lv: nc.gpsimd.tensor_tensor(out=gv(No[m2]), in0=gv(X0), in1=mkl(0, lv + 1), op=ALU.mult),
                          reads=[Bf["X0"], self.b_masks], writes=[Bf[nNo[m2]]])
                fw.op(fw.pool, lambda m2=m2, lv=lv: nc.gpsimd.tensor_tensor(out=gv(Mo[m2]), in0=gv(Y0), in1=mkl(1, lv + 1), op=ALU.mult),
                      reads=[Bf["Y0"], self.b_masks], writes=[Bf[nMo[m2]]])
            yield
            if not last:
                fw.op(fw.act, lambda: nc.scalar.copy(g2(Q[0]), P[pa][0:C, 0:G * C]), reads=[bP[pa]], writes=[Bf["Q0"]])
            fw.op(fw.dve, lambda: nc.vector.tensor_copy(g2(Q[1]), P[pb_][0:C, 0:G * C]), reads=[bP[pb_]], writes=[Bf["Q1"]])
            idm = self.cbf[0:C, 0:C]
            if not last:
                for hh in range(G):
                    fw.op(fw.pe, lambda hh=hh, cur=cur: nc.tensor.matmul(P[pa][0:C, hh * C:(hh + 1) * C], idm, hsl(A[cur], hh), start=True, stop=False),
                          reads=[Bf[nA[cur]], self.b_cbf], writes=[bP[pa]], inc=False)
                    fw.op(fw.pe, lambda hh=hh, cur=cur: nc.tensor.matmul(P[pa][0:C, hh * C:(hh + 1) * C], hsl(B[cur], hh), hsl(Q[0], hh), start=False, stop=True),
                          reads=[Bf[nB[cur]], Bf["Q0"]], writes=[bP[pa]], inc=(hh == G - 1))
            for hh in range(G):
                fw.op(fw.pe, lambda hh=hh, cur=cur: nc.tensor.matmul(P[pb_][0:C, hh * C:(hh + 1) * C], idm, hsl(B[cur], hh), start=True, stop=False),
                      reads=[Bf[nB[cur]], self.b_cbf], writes=[bP[pb_]], inc=False)
                fw.op(fw.pe, lambda hh=hh, cur=cur: nc.tensor.matmul(P[pb_][0:C, hh * C:(hh + 1) * C], hsl(A[cur], hh), hsl(Q[1], hh), start=False, stop=True),
                      reads=[Bf[nA[cur]], Bf["Q1"]], writes=[bP[pb_]], inc=(hh == G - 1))
            yield
            if not last:
                fw.op(fw.act, lambda cur=cur, nx=nx: nc.scalar.copy(g2(A[nx]), P[pa][0:C, 0:G * C]), reads=[bP[pa]], writes=[Bf[nA[nx]]])
                fw.op(fw.dve, lambda cur=cur, nx=nx: nc.vector.tensor_copy(g2(B[nx]), P[pb_][0:C, 0:G * C]), reads=[bP[pb_]], writes=[Bf[nB[nx]]])
            else:
                fw.op(fw.act, lambda cur=cur: nc.scalar.copy(g2(ATp), P[pb_][0:C, 0:G * C]), reads=[bP[pb_]], writes=[bATp])
            cur = nx
            yield

    def gdn_phaseB(self, tl, g, n_, chunk):
        fw, nc = self.fw, self.nc
        co, sg, cfirst, clast = chunk
        C = tl["C"]
        G = 4
        gb = self.gb
        P, bP = self.P, self.b_P
        par = n_ % 4
        V = self._gviews(tl, g, n_)
        hv, h2, hsl, colb, psv = V["hv"], V["h2"], V["hsl"], V["colb"], V["psv"]
        ATp, QKDp = self.ATs[par], self.QKDs[par]
        bATp, bQKDp, bdl = gb[f"AT{par}"], gb[f"QKD{par}"], gb[f"dl{par}"]
        dlast, gtot = self.dlasts[par], self.gtots[par]
        Sg = self.S[:, 4 * g:4 * g + 4, :]
        bS = self.b_S[g]
        hs = sg["seq"]
        if cfirst and tl["kind"] == "s":
            fw.dma(fw.sp, Sg, self.sgdn[hs - 1, 4 * g:4 * g + 4].rearrange("h k v -> k h v"), self.d_S[g], writes=[bS])
        fw.op(fw.dve, lambda: nc.vector.tensor_copy(self.Sbf[:, 0:G * HD].rearrange("p (g v) -> p g v", g=G), Sg), reads=[bS], writes=[gb["Sbf"]])
        fw.op(fw.pool, lambda: nc.gpsimd.tensor_tensor(out=Sg, in0=Sg, in1=bass.AP(gtot, 0, [[G, 128], [1, G], [0, HD]]), op=ALU.mult),
              reads=[bS, bdl], writes=[bS])
        for jj in range(2):
            kT = self.CH[:, 18 + jj, co:co + C]
            qT = self.CH[:, 16 + jj, co:co + C]
            srhs = self.Sbf[:, 2 * jj * HD:(2 * jj + 2) * HD]
            fw.op(fw.pe, lambda jj=jj, kT=kT, srhs=srhs: nc.tensor.matmul(P[3][0:C, jj * 256:(jj + 1) * 256], kT, srhs, start=True, stop=True),
                  reads=[self.b_CH[18 + jj], gb["Sbf"]], writes=[bP[3]], inc=(jj == 1))
            fw.op(fw.pe, lambda jj=jj, qT=qT, srhs=srhs: nc.tensor.matmul(P[4][0:C, jj * 256:(jj + 1) * 256], qT, srhs, start=True, stop=True),
                  reads=[self.b_CH[16 + jj], gb["Sbf"]], writes=[bP[4]], inc=(jj == 1))
        for hh in range(G):
            fw.op(fw.pe, lambda hh=hh: nc.tensor.transpose(self.PB[0:C, 512 + hh * HD:512 + (hh + 1) * HD], self.CH[:, 20 + hh, co:co + C], self.cbf[:, 0:128]),
                  reads=[self.b_CH[20 + hh], self.b_cbf], writes=[self.b_PB2], inc=(hh == G - 1))
        kS = psv(P[3], HD)
        qS = psv(P[4], HD)
        vtok = self.PB[0:C, 512:512 + G * HD].rearrange("p (g v) -> p g v", g=G)
        fw.op(fw.dve, lambda: nc.vector.tensor_tensor(out=hv(self.t1), in0=kS, in1=colb(self.egc, HD), op=ALU.mult), reads=[bP[3], gb["egc"]], writes=[gb["t1"]])
        fw.op(fw.dve, lambda: nc.vector.tensor_tensor(out=hv(self.t1), in0=vtok, in1=hv(self.t1), op=ALU.subtract), reads=[self.b_PB2, gb["t1"]], writes=[gb["t1"]])
        yield
        fw.op(fw.dve, lambda: nc.vector.tensor_tensor(out=hv(self.rr), in0=hv(self.t1), in1=colb(self.beta, HD), op=ALU.mult), reads=[gb["t1"], gb["beta"]], writes=[gb["rr"]])
        for jj in range(2):
            fw.op(fw.pe, lambda jj=jj: nc.tensor.transpose(self.PB[0:C, 512 + jj * HD:512 + (jj + 1) * HD], self.CH[:, 18 + jj, co:co + C], self.cbf[:, 0:128]),
                  reads=[self.b_CH[18 + jj], self.b_cbf], writes=[self.b_PB2], inc=(jj == 1))
        fw.op(fw.act, lambda: nc.scalar.copy(self.ktok[0:C, :], self.PB[0:C, 512:512 + 2 * HD]), reads=[self.b_PB2], writes=[gb["ktok"]])
        for hh in range(G):
            fw.op(fw.pe, lambda hh=hh: nc.tensor.matmul(P[5][0:C, hh * HD:(hh + 1) * HD], hsl(ATp, hh), hsl(self.rr, hh, HD), start=True, stop=True),
                  reads=[bATp, gb["rr"]], writes=[bP[5]], inc=(hh == G - 1))
        yield
        fw.op(fw.dve, lambda: nc.vector.tensor_tensor(out=hv(self.vnd), in0=psv(P[5], HD), in1=bass.AP(dlast, 0, [[G, C], [1, G], [0, HD]]), op=ALU.mult),
              reads=[bP[5], bdl], writes=[gb["vnd"]])
        fw.op(fw.act, lambda: nc.scalar.copy(h2(self.vn), P[5][0:C, 0:G * HD]), reads=[bP[5]], writes=[gb["vn"]])
        for jj in range(2):
            fw.op(fw.pe, lambda jj=jj: nc.tensor.matmul(P[5][:, jj * 256:(jj + 1) * 256], self.ktok[0:C, jj * HD:(jj + 1) * HD],
                                                        self.vnd[0:C, 2 * jj * HD:(2 * jj + 2) * HD], start=True, stop=True),
                  reads=[gb["ktok"], gb["vnd"]], writes=[bP[5]], inc=(jj == 1))
        for hh in range(G):
            fw.op(fw.pe, lambda hh=hh: nc.tensor.matmul(P[3][0:C, hh * HD:(hh + 1) * HD], hsl(QKDp, hh), hsl(self.vn, hh, HD), start=True, stop=True),
                  reads=[bQKDp, gb["vn"]], writes=[bP[3]], inc=(hh == G - 1))
        yield
        fw.op(fw.dve, lambda: nc.vector.tensor_tensor(out=Sg, in0=Sg, in1=P[5][:, 0:G * HD].rearrange("p (g v) -> p g v", g=G), op=ALU.add),
              reads=[bS, bP[5]], writes=[bS])
        if clast and sg["last"]:
            dst = self.o_sp if tl["kind"] == "p" else self.o_ss[hs - 1]
            ob = Buf()
            fw.dma(fw.pool, dst[4 * g:4 * g + 4].rearrange("h k v -> k h v"), Sg, self.d_S[g], reads=[bS], writes=[ob])
            self.outbufs.append(ob)
        fw.op(fw.dve, lambda: nc.vector.tensor_tensor(out=hv(self.t2), in0=qS, in1=colb(self.egc, HD), op=ALU.mult), reads=[bP[4], gb["egc"]], writes=[gb["t2"]])
        yield
        fw.op(fw.dve, lambda: nc.vector.tensor_tensor(out=h2(self.oo), in0=h2(self.t2), in1=P[3][0:C, 0:G * HD], op=ALU.add),
              reads=[bP[3], gb["t2"]], writes=[gb["oo"]])
        fw.op(fw.act, lambda: nc.scalar.activation(h2(self.osq), h2(self.oo), AF.Square), reads=[gb["oo"]], writes=[gb["osq"]])
        fw.op(fw.dve, lambda: nc.vector.tensor_reduce(out=self.oss[0:C, 0, :], in_=hv(self.osq), axis=AX.X, op=ALU.add), reads=[gb["osq"]], writes=[gb["oss"]])
        fw.op(fw.act, lambda: nc.scalar.activation(self.oss[0:C, 1, :], self.oss[0:C, 0, :], AF.Ln, bias=self.sc[0:C, 0:1], scale=1.0 / HD),
              reads=[gb["oss"], self.b_sc], writes=[gb["oss"]])
        fw.op(fw.act, lambda: nc.scalar.activation(self.oss[0:C, 1, :], self.oss[0:C, 1, :], AF.Exp, scale=-0.5), reads=[gb["oss"]], writes=[gb["oss"]])
        yield
        fw.op(fw.dve, lambda: nc.vector.tensor_tensor(out=hv(self.oo), in0=hv(self.oo), in1=bass.AP(self.oss, G, [[2 * G, C], [1, G], [0, HD]]), op=ALU.mult),
              reads=[gb["oo"], gb["oss"]], writes=[gb["oo"]])
        for hh in range(G):
            fw.op(fw.pe, lambda hh=hh: nc.tensor.transpose(P[4][:, hh * C:(hh + 1) * C], hsl(self.oo, hh, HD), self.cst[0:C, 0:C]),
                  reads=[gb["oo"], self.b_cst], writes=[bP[4]], inc=(hh == G - 1))
        yield
        for hh in range(G):
            h = 4 * g + hh
            fw.op(fw.dve, lambda hh=hh, h=h: nc.vector.scalar_tensor_tensor(
                out=self.CH[:, h, co:co + C], in0=P[4][:, hh * C:(hh + 1) * C], scalar=self.V("g_norm", 0), in1=self.CH[:, 24 + hh, co:co + C],
                op0=ALU.mult, op1=ALU.mult), reads=[bP[4], self.b_vecs, self.b_CH[24 + hh]], writes=[self.b_CH[h]])

    def load_ghalo(self, sg):
        fw, nc = self.fw, self.nc
        hs = sg["seq"]
        HL = SCW - 1
        for q in range(4):
            fw.dma(fw.sp, self.cin[0:HL, :], self.cgc[hs - 1, :, q * 1024:(q + 1) * 1024], self.d_cin, writes=[self.b_cin])
            P = self.P[6]
            for c in range(8):
                fw.op(fw.pe, lambda c=c: nc.tensor.transpose(P[:, c * 4:c * 4 + HL], self.cin[0:HL, c * 128:(c + 1) * 128], self.ident[0:HL, 0:HL]),
                      reads=[self.b_cin, self.b_cst], writes=[self.b_P[6]], inc=(c == 7))
            fw.op(fw.dve, lambda q=q: nc.vector.tensor_copy(self.ghalo[:, hs, q * 8:(q + 1) * 8, :], P[:, 0:32].rearrange("p (c h) -> p c h", c=8)[:, :, 0:HL]),
                  reads=[self.b_P[6]], writes=self.b_ghalo[hs][q * 8:(q + 1) * 8])


_CACHE = {}


def _get_prog(n_ptiles=16, do_sample=True):
    key = (n_ptiles, do_sample)
    if key not in _CACHE:
        _CACHE[key] = Prog(n_ptiles, do_sample)
    return _CACHE[key]


def make_in_maps(inp):
    wall = build_wall(inp)
    vecs = build_vecs(inp)
    consts = build_consts()
    masks = build_masks(128)
    hrow = np.concatenate([np.asarray(inp["gdn_A_log"][0], np.float32), np.asarray(inp["gdn_dt_bias"][0], np.float32)])[None, :]
    maps = []
    for i in range(8):
        s = slice(NSMP * i, NSMP * i + NSMP)
        maps.append({
            "xp": np.ascontiguousarray(inp["x_prompt"][i]),
            "xs": np.ascontiguousarray(np.asarray(inp["x_sample"][s]).reshape(NSMP * DSEQ, D)),
            "c5": np.ascontiguousarray(np.concatenate([np.asarray(inp["c_prompt"][i:i + 1]), np.asarray(inp["c_sample"][s])], 0)),
            "cconv": np.ascontiguousarray(inp["cache_conv"][0, s]),
            "sgdn": np.ascontiguousarray(inp["state_gdn"][0, s]),
            "cgc": np.ascontiguousarray(inp["cache_gdn_conv"][0, s]),
            "wall": wall, "vecs": vecs, "hrow": np.ascontiguousarray(hrow), "consts": consts, "masks": masks,
        })
    return maps


def kernel(**inp):
    inp = {k: np.asarray(v) for k, v in inp.items()}
    prog = _get_prog()
    maps = make_in_maps(inp)
    res = run_bass_kernel_spmd(prog.nc, maps, core_ids=list(range(8)))
    R = res.results
    f = np.float32
    y_p = np.stack([R[i]["yp"] for i in range(8)]).astype(f)
    y_s = np.concatenate([R[i]["ys"].reshape(NSMP, DSEQ, D) for i in range(8)]).astype(f)
    ccp = np.stack([R[i]["ccp"] for i in range(8)])[None].astype(f)
    ccs = np.concatenate([R[i]["ccs"] for i in range(8)])[None].astype(f)
    s_p = np.stack([R[i]["sp"] for i in range(8)])[None].astype(f)
    s_s = np.concatenate([R[i]["ss"] for i in range(8)])[None].astype(f)
    gcp = np.stack([R[i]["gcp"] for i in range(8)])[None].astype(f)
    gcs = np.concatenate([R[i]["gcs"] for i in range(8)])[None].astype(f)
    return (y_p, y_s, ccp, ccs, s_p, s_s, gcp, gcs)
```
